# Optimizing a Trainium2 kernel written in Bass

```python
import math
import jax, jax.numpy as jnp
from jax import lax
import numpy as np

D_MODEL = 1024
BATCH = 4
SEQ = 4096
DEPTH = 2
DEC_BATCH = 128
DEC_SEQ = 4
PAST_LEN = 8192
PAGE_SIZE = 128

N_EVEN = (DEPTH + 1) // 2
N_ODD = DEPTH // 2

MLA_HEADS = 8
NOPE_DIM = 64
ROPE_DIM = 32
V_DIM = 64
Q_RANK = 256
KV_RANK = 128
ROPE_THETA = 10000.0
Q_BLOCK = 128

HG_HEADS = 4
HG_K = 128
HG_V = 128
HG_WIDTH = HG_HEADS * HG_K
HG_CHUNK = 64

GM_CHUNK = 128
GM_GROUPS = 4
GM_GROUP_DIM = 128
GM_WIDTH = GM_GROUPS * GM_GROUP_DIM

SC_WIDTH = 512
CONV_W = 3

D_FF = 2816

ALPHA = (2 * DEPTH) ** 0.25
BETA = (8 * DEPTH) ** -0.25
LN_EPS = 1e-5
RMS_EPS = 1e-6

IN_AB = Q_RANK + KV_RANK + ROPE_DIM + 4 * HG_WIDTH
MIX_AB = MLA_HEADS * V_DIM + HG_WIDTH
IN_CD = 2 * GM_WIDTH + 3 * SC_WIDTH
MIX_CD = GM_WIDTH + SC_WIDTH

kernel_name = 'hybrid_mla_hgrn2_gmlp_shortconv_decoder_step'

F32 = jnp.float32


def rms_norm(x, g):
    xf = x.astype(F32)
    y = xf * lax.rsqrt(jnp.mean(xf * xf, axis=-1, keepdims=True) + RMS_EPS)
    return (y * g.astype(F32)).astype(x.dtype)


def layer_norm(x, g, b):
    xf = x.astype(F32)
    xc = xf - jnp.mean(xf, axis=-1, keepdims=True)
    var = jnp.mean(xc * xc, axis=-1, keepdims=True)
    return (xc * lax.rsqrt(var + LN_EPS) * g.astype(F32) + b.astype(F32)).astype(x.dtype)


def rope_tables(offset, length):
    pos = (offset + jnp.arange(length)).astype(F32)
    inv = ROPE_THETA ** (-jnp.arange(0, ROPE_DIM, 2, dtype=F32) / ROPE_DIM)
    ang = pos[:, None] * inv[None, :]
    return jnp.cos(ang), jnp.sin(ang)


def apply_rope(x, cos, sin):
    xf = x.astype(F32)
    x1, x2 = jnp.split(xf, 2, axis=-1)
    return jnp.concatenate([x1 * cos - x2 * sin, x1 * sin + x2 * cos], axis=-1).astype(x.dtype)


def causal_conv(x, prefix, w):
    L = x.shape[1]
    xp = jnp.concatenate([prefix.astype(x.dtype), x], axis=1)
    y = w[0] * xp[:, 0:L]
    for i in range(1, CONV_W):
        y = y + w[i] * xp[:, i:i + L]
    return y, xp[:, L:]


def mla_attend(q_lat, q_rope, c_kv, k_rope, q_offset):
    Bn, Lq, H, C = q_lat.shape
    qb = min(Q_BLOCK, Lq)
    Lp = -(-Lq // qb) * qb
    if Lp != Lq:
        pw = ((0, 0), (0, Lp - Lq), (0, 0), (0, 0))
        q_lat = jnp.pad(q_lat, pw)
        q_rope = jnp.pad(q_rope, pw)
    k_pos = jnp.arange(c_kv.shape[1])
    scale = (NOPE_DIM + ROPE_DIM) ** -0.5

    def block(i):
        start = i * qb
        ql = lax.dynamic_slice_in_dim(q_lat, start, qb, axis=1)
        qr = lax.dynamic_slice_in_dim(q_rope, start, qb, axis=1)
        s = (jnp.einsum('bqhc,bkc->bhqk', ql, c_kv)
             + jnp.einsum('bqhr,bkr->bhqk', qr, k_rope)).astype(F32) * scale
        q_pos = q_offset + start + jnp.arange(qb)
        s = jnp.where(k_pos[None, :] <= q_pos[:, None], s, -jnp.inf)
        p = jax.nn.softmax(s, axis=-1).astype(c_kv.dtype)
        return jnp.einsum('bhqk,bkc->bqhc', p, c_kv)

    o = lax.map(block, jnp.arange(Lp // qb))
    return o.swapaxes(0, 1).reshape(Bn, Lp, H, C)[:, :Lq]


def hgrn2_recurrence(q, k, v, log_f, s0):
    Bn, L, H, K = q.shape
    C = min(HG_CHUNK, L)
    Lp = -(-L // C) * C
    if Lp != L:
        pw = ((0, 0), (0, Lp - L), (0, 0), (0, 0))
        q, k, v, log_f = jnp.pad(q, pw), jnp.pad(k, pw), jnp.pad(v, pw), jnp.pad(log_f, pw)
    n = Lp // C

    def chunks(a):
        return a.reshape(Bn, n, C, H, a.shape[-1]).swapaxes(0, 1)

    mask = jnp.tril(jnp.ones((C, C), dtype=bool))[None, :, :, None, None]

    def step(S, xs):
        qc, kc, vc, lfc = xs
        b = jnp.cumsum(lfc, axis=1)
        diff = b[:, :, None] - b[:, None, :]
        decay = jnp.exp(jnp.where(mask, diff, -jnp.inf))
        scores = jnp.einsum('bthk,btshk,bshk->bhts', qc, decay, kc)
        o = (jnp.einsum('bhts,bshv->bthv', scores, vc)
             + jnp.einsum('bthk,bhkv->bthv', qc * jnp.exp(b), S))
        b_last = b[:, -1]
        S_new = (jnp.exp(b_last)[..., None] * S
                 + jnp.einsum('bshk,bshv->bhkv', kc * jnp.exp(b_last[:, None] - b), vc))
        return S_new, o

    S, o = lax.scan(step, s0, (chunks(q), chunks(k), chunks(v), chunks(log_f)))
    o = o.swapaxes(0, 1).reshape(Bn, Lp, H, v.shape[-1])[:, :L]
    return o, S


def chunk_spatial_mix(v, ws, bs):
    Bn, L, _ = v.shape
    Lp = -(-L // GM_CHUNK) * GM_CHUNK
    if Lp != L:
        v = jnp.pad(v, ((0, 0), (0, Lp - L), (0, 0)))
    vc = v.reshape(Bn, Lp // GM_CHUNK, GM_CHUNK, GM_GROUPS, GM_GROUP_DIM)
    wm = jnp.where(jnp.tril(jnp.ones((GM_CHUNK, GM_CHUNK), dtype=bool))[None], ws, 0)
    out = jnp.einsum('gts,bnsgc->bntgc', wm, vc) + bs.T[None, None, :, :, None]
    return out.reshape(Bn, Lp, GM_WIDTH)[:, :L]


def ab_mixer(x, offset, lb, pe, past):
    w_in, g_q, w_uq, g_kv, w_uk, w_uv, g_hg, w_out = pe
    Bn, L, _ = x.shape
    i1 = Q_RANK
    i2 = i1 + KV_RANK
    i3 = i2 + ROPE_DIM
    i4 = i3 + HG_WIDTH
    i5 = i4 + HG_WIDTH
    i6 = i5 + HG_WIDTH
    cq, ckv, kr, hf, hi, hq, hg = jnp.split(x @ w_in, [i1, i2, i3, i4, i5, i6], axis=-1)

    cos, sin = rope_tables(offset, L)
    q = (rms_norm(cq, g_q) @ w_uq).reshape(Bn, L, MLA_HEADS, NOPE_DIM + ROPE_DIM)
    q_nope = q[..., :NOPE_DIM]
    q_rope = apply_rope(q[..., NOPE_DIM:], cos[:, None], sin[:, None])
    ckv = rms_norm(ckv, g_kv)
    kr = apply_rope(kr, cos, sin)
    q_lat = jnp.einsum('blhd,chd->blhc', q_nope, w_uk)
    if past is None:
        keys_c, keys_r = ckv, kr
        s0 = jnp.zeros((Bn, HG_HEADS, HG_K, HG_V), F32)
    else:
        lat_past, kr_past, hg_past = past
        keys_c = jnp.concatenate([lat_past.astype(ckv.dtype), ckv], axis=1)
        keys_r = jnp.concatenate([kr_past.astype(kr.dtype), kr], axis=1)
        s0 = hg_past.astype(F32)
    o_lat = mla_attend(q_lat, q_rope, keys_c, keys_r, offset)
    mla_out = jnp.einsum('blhc,chd->blhd', o_lat, w_uv).reshape(Bn, L, MLA_HEADS * V_DIM)

    lbf = lb.astype(F32)
    hf32 = hf.astype(F32)
    log_f = jnp.logaddexp(jnp.log(lbf), jnp.log1p(-lbf) + jax.nn.log_sigmoid(hf32))
    k_in = (1.0 - lbf) * jax.nn.sigmoid(-hf32)

    def heads(a):
        return a.reshape(Bn, L, HG_HEADS, -1)

    o_hg, s_fin = hgrn2_recurrence(heads(hq.astype(F32)), heads(k_in), heads(hi.astype(F32)),
                                   heads(log_f), s0)
    o_hg = rms_norm(o_hg, g_hg) * jax.nn.silu(heads(hg.astype(F32)))
    o_hg = o_hg.reshape(Bn, L, HG_WIDTH).astype(x.dtype)

    y = jnp.concatenate([mla_out, o_hg], axis=-1) @ w_out
    return y, ckv, kr, s_fin


def cd_mixer(x, po, conv_prefix):
    w_in, ln_g, ln_b, ws, bs, conv_w, w_out = po
    s1 = GM_WIDTH
    s2 = 2 * GM_WIDTH
    s3 = s2 + SC_WIDTH
    s4 = s3 + SC_WIDTH
    u, v, gate_b, gate_c, h = jnp.split(x @ w_in, [s1, s2, s3, s4], axis=-1)

    u = jax.nn.gelu(u)
    v = layer_norm(jax.nn.gelu(v), ln_g, ln_b)
    c_out = u * chunk_spatial_mix(v, ws, bs)

    conv_out, conv_state = causal_conv(gate_c * h, conv_prefix, conv_w)
    d_out = gate_b * conv_out

    y = jnp.concatenate([c_out, d_out], axis=-1) @ w_out
    return y, v, conv_state


def conv_ffn(x, w_up, conv_w, w_down, prefix):
    up, state = causal_conv(x @ w_up, prefix, conv_w)
    a, b = jnp.split(up, 2, axis=-1)
    return (jax.nn.silu(a) * b) @ w_down, state


def trunk(x, offset, lb_all, even_p, odd_p, layer_p, past):
    Bn = x.shape[0]
    ln1_g, ln1_b, w_up, ffn_conv_w, w_down, ln2_g, ln2_b = layer_p
    lat_rows, kr_rows, hg_states, gm_rows, sc_states, ffn_states = [], [], [], [], [], []
    for l in range(DEPTH):
        j = l // 2
        if l % 2 == 0:
            pe = tuple(a[j] for a in even_p)
            if past is None:
                ab_past = None
            else:
                cache_latent, cache_krope, page_table, state_hgrn = past[0], past[1], past[2], past[3]
                lat_past = cache_latent[j][page_table].reshape(Bn, -1, KV_RANK)
                kr_past = cache_krope[j][page_table].reshape(Bn, -1, ROPE_DIM)
                ab_past = (lat_past, kr_past, state_hgrn[j])
            y, c_new, r_new, s_new = ab_mixer(x, offset, lb_all[j], pe, ab_past)
            lat_rows.append(c_new)
            kr_rows.append(r_new)
            hg_states.append(s_new.astype(x.dtype))
        else:
            po = tuple(a[j] for a in odd_p)
            if past is None:
                sc_prefix = jnp.zeros((Bn, CONV_W - 1, SC_WIDTH), x.dtype)
            else:
                sc_prefix = past[4][j]
            y, v_rows, sc_new = cd_mixer(x, po, sc_prefix)
            if past is not None:
                gm_rows.append(v_rows)
            sc_states.append(sc_new)
        x = layer_norm(ALPHA * x + y, ln1_g[l], ln1_b[l])
        if past is None:
            f_prefix = jnp.zeros((Bn, CONV_W - 1, 2 * D_FF), x.dtype)
        else:
            f_prefix = past[5][l]
        f, f_new = conv_ffn(x, w_up[l], ffn_conv_w[l], w_down[l], f_prefix)
        ffn_states.append(f_new)
        x = layer_norm(ALPHA * x + f, ln2_g[l], ln2_b[l])
    gm = jnp.stack(gm_rows) if gm_rows else None
    return (x, jnp.stack(lat_rows), jnp.stack(kr_rows), jnp.stack(hg_states), gm,
            jnp.stack(sc_states), jnp.stack(ffn_states))


def setup_inputs(seed: int = 0) -> dict:
    key = jax.random.key(seed)
    ks = iter(jax.random.split(key, 64))

    def nrm(shape, scale):
        return jax.random.normal(next(ks), shape, F32) * scale

    def gain(shape):
        return 1.0 + nrm(shape, 0.01)

    n_pages = PAST_LEN // PAGE_SIZE
    n_used = DEC_BATCH * n_pages
    n_pool = n_used + max(1, n_used // 4)
    page_table = jax.random.permutation(next(ks), n_pool)[:n_used].reshape(DEC_BATCH, n_pages).astype(jnp.int32)

    return {
        'x_prompt': nrm((BATCH, SEQ, D_MODEL), 1.0),
        'x_sample': nrm((DEC_BATCH, DEC_SEQ, D_MODEL), 1.0),
        'cache_latent': nrm((N_EVEN, n_pool, PAGE_SIZE, KV_RANK), 1.0),
        'cache_krope': nrm((N_EVEN, n_pool, PAGE_SIZE, ROPE_DIM), 1.0),
        'state_hgrn': nrm((N_EVEN, DEC_BATCH, HG_HEADS, HG_K, HG_V), 0.5),
        'state_shortconv': nrm((N_ODD, DEC_BATCH, CONV_W - 1, SC_WIDTH), 1.0),
        'state_ffn_conv': nrm((DEPTH, DEC_BATCH, CONV_W - 1, 2 * D_FF), 1.0),
        'page_table': page_table,
        'w_in_ab': nrm((N_EVEN, D_MODEL, IN_AB), D_MODEL ** -0.5),
        'g_qnorm': gain((N_EVEN, Q_RANK)),
        'w_uq': nrm((N_EVEN, Q_RANK, MLA_HEADS * (NOPE_DIM + ROPE_DIM)), Q_RANK ** -0.5),
        'g_kvnorm': gain((N_EVEN, KV_RANK)),
        'w_uk': nrm((N_EVEN, KV_RANK, MLA_HEADS, NOPE_DIM), KV_RANK ** -0.5),
        'w_uv': nrm((N_EVEN, KV_RANK, MLA_HEADS, V_DIM), KV_RANK ** -0.5),
        'hg_lb': nrm((N_EVEN + 1, HG_WIDTH), 0.1),
        'g_hgnorm': gain((N_EVEN, HG_V)),
        'w_out_ab': nrm((N_EVEN, MIX_AB, D_MODEL), BETA * MIX_AB ** -0.5),
        'w_in_cd': nrm((N_ODD, D_MODEL, IN_CD), D_MODEL ** -0.5),
        'gm_ln_g': gain((N_ODD, GM_WIDTH)),
        'gm_ln_b': nrm((N_ODD, GM_WIDTH), 0.01),
        'gm_ws': nrm((N_ODD, GM_GROUPS, GM_CHUNK, GM_CHUNK), 0.5 * GM_CHUNK ** -0.5),
        'gm_bs': gain((N_ODD, GM_GROUPS, GM_CHUNK)),
        'sc_conv': nrm((N_ODD, CONV_W, SC_WIDTH), CONV_W ** -0.5),
        'w_out_cd': nrm((N_ODD, MIX_CD, D_MODEL), BETA * MIX_CD ** -0.5),
        'ln1_g': gain((DEPTH, D_MODEL)),
        'ln1_b': nrm((DEPTH, D_MODEL), 0.01),
        'ffn_w_up': nrm((DEPTH, D_MODEL, 2 * D_FF), D_MODEL ** -0.5),
        'ffn_conv': nrm((DEPTH, CONV_W, 2 * D_FF), CONV_W ** -0.5),
        'ffn_w_down': nrm((DEPTH, D_FF, D_MODEL), BETA * D_FF ** -0.5),
        'ln2_g': gain((DEPTH, D_MODEL)),
        'ln2_b': nrm((DEPTH, D_MODEL), 0.01),
    }


def reference(x_prompt, x_sample, cache_latent, cache_krope, state_hgrn, state_shortconv,
              state_ffn_conv, page_table, w_in_ab, g_qnorm, w_uq, g_kvnorm, w_uk, w_uv, hg_lb,
              g_hgnorm, w_out_ab, w_in_cd, gm_ln_g, gm_ln_b, gm_ws, gm_bs, sc_conv, w_out_cd,
              ln1_g, ln1_b, ffn_w_up, ffn_conv, ffn_w_down, ln2_g, ln2_b):
    lb_all = jnp.cumsum(jax.nn.softmax(hg_lb.astype(F32), axis=0), axis=0)[:N_EVEN]
    even_p = (w_in_ab, g_qnorm, w_uq, g_kvnorm, w_uk, w_uv, g_hgnorm, w_out_ab)
    odd_p = (w_in_cd, gm_ln_g, gm_ln_b, gm_ws, gm_bs, sc_conv, w_out_cd)
    layer_p = (ln1_g, ln1_b, ffn_w_up, ffn_conv, ffn_w_down, ln2_g, ln2_b)

    y_prompt, lat_p, kr_p, hg_p, _, sc_p, ffn_p = trunk(
        x_prompt, 0, lb_all, even_p, odd_p, layer_p, None)
    past = (cache_latent, cache_krope, page_table, state_hgrn, state_shortconv, state_ffn_conv)
    y_sample, lat_s, kr_s, hg_s, gmv_s, sc_s, ffn_s = trunk(
        x_sample, PAST_LEN, lb_all, even_p, odd_p, layer_p, past)
    return (y_prompt, y_sample, lat_p, kr_p, lat_s, kr_s, hg_p, hg_s, gmv_s, sc_p, sc_s, ffn_p, ffn_s)
```

```python
import numpy as np
import concourse.bass as bass
import concourse.mybir as mybir
from concourse.bass_utils import run_bass_kernel_spmd
from contextlib import ExitStack

F32 = mybir.dt.float32
BF16 = mybir.dt.bfloat16
I32 = mybir.dt.int32
AF = mybir.ActivationFunctionType
ALU = mybir.AluOpType
AX = mybir.AxisListType

NXM = 2304
NSM = 64
NX = NXM + NSM
EXT0 = 1792
MAIN_TILES = [(0, 256), (256, 512), (768, 512), (1280, 512), (1792, 512)]
PRE_TILES = [(0, 512), (512, 512), (1024, 512), (1536, 256)]
FFN_TILES = [(0, 768), (768, 1536), (1536, 2368)]
ALPHA = 4.0 ** 0.25
SCALE = 96.0 ** -0.5
LN_EPS = 1e-5
RMS_EPS = 1e-6
NPOOL = 10240
DBG_COMPACT = False
DBG_CORES = None
DBG_S2 = 99
DBG_M = 99
DBG_SKIPKR = False
DBG_SKIPLAT = False
DBG_PHASES = 99

C_GQ = 0
C_GKV = 2
C_GHG = 3
C_HGLB = 4
C_LN1G = 12
C_LN1B = 28
C_LN2G = 44
C_LN2B = 60
C_FCW = 76
C_SCW = 340
NCOLS = 352


class Buf:
    __slots__ = ("name", "w", "r")

    def __init__(self, name):
        self.name = name
        self.w = None
        self.r = []


class Sched:
    ENGS = ["pe", "act", "dve", "pool", "sp"]

    def __init__(self, nc, es):
        self.nc = nc
        self.es = es
        self.ops = {e: [] for e in self.ENGS}
        self.waited = {e: {} for e in self.ENGS}
        self.nsem = 0
        self.sems = {e: es.enter_context(nc.semaphore("s_" + e)) for e in self.ENGS if e != "sp"}
        self.dma_events = []
        self.sem_pool = []

    def new_dma_sem(self, name):
        self.nsem += 1
        return [self.es.enter_context(self.nc.semaphore("d%d_%s" % (self.nsem, name))), 0]

    def _filter(self, eng, deps):
        out = []
        for d in deps:
            if d[0] == "eng":
                if d[1] == eng and eng == "pe":
                    continue
                if self.waited[eng].get(d[1], -1) >= d[2]:
                    continue
                self.waited[eng][d[1]] = d[2]
                self.ops[d[1]][d[2]]["inc"] = True
                out.append(d)
            else:
                key = ("dma", id(d[1]))
                if self.waited[eng].get(key, -1) >= d[2]:
                    continue
                self.waited[eng][key] = d[2]
                out.append(d)
        return out

    def _deps(self, eng, reads, writes):
        deps = []
        for b in reads:
            if b.w is not None:
                deps.append(b.w)
        for b in writes:
            if b.w is not None:
                deps.append(b.w)
            deps.extend(b.r)
        return self._filter(eng, deps)

    def op(self, eng, fn, reads=(), writes=()):
        deps = self._deps(eng, reads, writes)
        idx = len(self.ops[eng])
        self.ops[eng].append({"fn": fn, "deps": deps, "inc": False, "dma": None})
        ev = ("eng", eng, idx)
        for b in writes:
            b.w = ev
            b.r = []
        for b in reads:
            if b.w is not ev:
                b.r.append(ev)
        return ev

    def dma(self, eng, fn, sem, reads=(), writes=(), count=1):
        deps = self._deps(eng, reads, writes)
        if sem[1] > 0:
            deps = deps + self._filter(eng, [("dma", sem[0], sem[1])])
        sem[1] += 16 * count
        ev = ("dma", sem[0], sem[1])
        self.ops[eng].append({"fn": fn, "deps": deps, "inc": False, "dma": sem[0]})
        for b in writes:
            b.w = ev
            b.r = []
        for b in reads:
            b.r.append(ev)
        self.dma_events.append(ev)
        return ev

    def wait_events(self, eng, evs):
        deps = self._filter(eng, evs)
        self.ops[eng].append({"fn": None, "deps": deps, "inc": False, "dma": None})

    def barrier(self):
        evs = []
        for e in self.ENGS:
            for i in range(len(self.ops[e]) - 1, -1, -1):
                o = self.ops[e][i]
                if o["fn"] is not None and o["dma"] is None:
                    evs.append(("eng", e, i))
                    break
        evs = evs + list(self.dma_events)
        self.dma_events = []
        for e in self.ENGS:
            self.wait_events(e, evs)

    def emit(self):
        nc = self.nc
        incidx = {}
        for e in self.ENGS:
            c = 0
            lst = []
            for o in self.ops[e]:
                if o["inc"]:
                    c += 1
                lst.append(c)
            incidx[e] = lst
        sems = self.sems

        def run(e, engobj):
            for o in self.ops[e]:
                for d in o["deps"]:
                    if d[0] == "eng":
                        engobj.wait_ge(sems[d[1]], incidx[d[1]][d[2]])
                    else:
                        engobj.wait_ge(d[1], d[2])
                if o["fn"] is None:
                    continue
                ins = o["fn"](engobj)
                if o["dma"] is not None:
                    if isinstance(ins, list):
                        for i_ in ins:
                            i_.then_inc(o["dma"], 16)
                    else:
                        ins.then_inc(o["dma"], 16)
                elif o["inc"]:
                    ins.then_inc(sems[e], 1)

        with nc.Block() as block:
            @block.tensor
            def _(t):
                run("pe", t)

            @block.scalar
            def _(t):
                run("act", t)

            @block.vector
            def _(t):
                run("dve", t)

            @block.gpsimd
            def _(t):
                run("pool", t)

            @block.sync
            def _(t):
                run("sp", t)


class Arena:
    def __init__(self, flat, nwords):
        self.flat = flat
        self.n = nwords
        self.off = 0

    def reset(self):
        self.off = 0

    def alloc(self, shape, dt):
        elems = 1
        for s_ in shape[1:]:
            elems *= s_
        words = elems if dt in (F32, I32) else (elems + 1) // 2
        v = self.flat[0:shape[0], self.off:self.off + words]
        self.off += words
        assert self.off <= self.n, "arena overflow"
        if dt not in (F32,):
            v = v.bitcast(dt)
            v = v[:, 0:elems]
        if len(shape) > 2:
            names = ["d%d" % i for i in range(len(shape) - 1)]
            pat = "p (%s) -> p %s" % (" ".join(names), " ".join(names))
            kw = {names[i]: shape[i + 1] for i in range(len(names) - 1)}
            v = v.rearrange(pat, **kw)
        return v


class Ring:
    def __init__(self, items):
        self.items = items
        self.i = 0

    def next(self):
        it = self.items[self.i % len(self.items)]
        self.i += 1
        return it


def build_program():
    nc = bass.Bass("TRN2", target_bir_lowering=False)

    def din(name, shape, dt=F32):
        return nc.dram_tensor(name, list(shape), dt, kind="ExternalInput").ap()

    def dout(name, shape, dt=F32):
        return nc.dram_tensor(name, list(shape), dt, kind="ExternalOutput").ap()

    xT = din("xT", [1024, 4096])
    xsT = din("xsT", [1024, NSM])
    cols_d = din("cols", [128, NCOLS])
    pcc_d = din("pcc", [128, 34])
    cmat_d = din("cmat", [6, 128, 128])
    mask8_d = din("mask8", [128, 8, 128])
    mask8s_d = din("mask8s", [64, 16 * 8 * 4])
    mask4s_d = din("mask4s", [64, 4, 64])
    ropeC_d = din("ropeC", [32, 4096])
    ropeS_d = din("ropeS", [32, 4096])
    ropeCs_d = din("ropeCs", [32, NSM])
    ropeSs_d = din("ropeSs", [32, NSM])
    rows_d = din("rows", [4, 512])
    ptab_d = din("ptab", [1, 1024], I32)
    clat_d = din("clat", [NPOOL, 128, 128])
    ckr_d = din("ckr", [NPOOL, 128, 32])
    shg_d = din("shg", [16, 4, 128, 128])
    ssc_d = din("ssc", [128, 4, 16, 2])
    sffn_d = din("sffn", [2, 128, 44, 16, 2])
    wab_d = din("wab", [5, 128, 8, 512])
    wuq_d = din("wuq", [128, 2, 1024])
    wukt_d = din("wukt", [128, 8, 128])
    wuv_d = din("wuv", [128, 8, 128])
    woab_d = din("woab", [128, 8, 1024])
    wcd_d = din("wcd", [5, 128, 8, 512])
    wocd_d = din("wocd", [128, 8, 1024])
    wmt_d = din("wmt", [128, 4, 128])
    wmts_d = din("wmts", [64, 4, 64])
    bsr_d = din("bsr", [1, 4, 128])
    bsrs_d = din("bsrs", [1, 4, 64])
    wup_d = din("wup", [2, 44, 128, 8, 128])
    wdn_d = din("wdn", [2, 8, 128, 22, 128])

    yT_o = dout("yT", [1024, 2048])
    ysT_o = dout("ysT", [1024, NSM])
    latp_o = dout("latp", [2048, 128])
    krp_o = dout("krp", [2048, 32])
    lats_o = dout("lats", [NSM, 128])
    krs_o = dout("krs", [NSM, 32])
    hgp_o = dout("hgp", [4, 128, 128])
    hgs_o = dout("hgs", [16, 4, 128, 128])
    gmv_o = dout("gmv", [NSM, 512])
    scp_o = dout("scp", [128, 4, 2])
    scs_o = dout("scs", [128, 4, 16, 2])
    ffp_o = dout("ffp", [2, 128, 44, 2])
    ffs_o = dout("ffs", [2, 128, 44, 16, 2])

    xTv = xT.rearrange("(kc p) n -> p kc n", p=128)
    xsTv = xsT.rearrange("(kc p) n -> p kc n", p=128)

    with ExitStack() as es:
        S = Sched(nc, es)

        def MM(out, lhsT, rhs, st, sp, R, W):
            S.op("pe", lambda e: e.matmul(out, lhsT=lhsT, rhs=rhs, start=st, stop=sp), R, W)

        def TR(out, in_, ident, R, W):
            S.op("pe", lambda e: e.transpose(out, in_, ident), R, W)

        def ACT(out, in_, func, R, W, bias=None, scale=None, accum=None):
            kw = {}
            if bias is not None:
                kw["bias"] = bias
            if scale is not None:
                kw["scale"] = scale
            if accum is not None:
                kw["accum_out"] = accum
            S.op("act", lambda e: e.activation(out=out, in_=in_, func=func, **kw), R, W)

        def TT(eng, out, in0, in1, op, R, W):
            S.op(eng, lambda e: e.tensor_tensor(out=out, in0=in0, in1=in1, op=op), R, W)

        def TS(eng, out, in0, s1, s2, op0, op1, R, W):
            if s2 is None:
                S.op(eng, lambda e: e.tensor_scalar(out=out, in0=in0, scalar1=s1, scalar2=None, op0=op0), R, W)
            else:
                S.op(eng, lambda e: e.tensor_scalar(out=out, in0=in0, scalar1=s1, scalar2=s2, op0=op0, op1=op1), R, W)

        def RSQRT(out, in_, eps_ap, R, W):
            S.op("act", lambda e: e.activation(out=out, in_=in_, func=AF.Sqrt, bias=eps_ap), R, W)
            S.op("dve", lambda e: e.reciprocal(out, out), W, W)

        def STT(eng, out, in0, sc, in1, op0, op1, R, W):
            S.op(eng, lambda e: e.scalar_tensor_tensor(out=out, in0=in0, scalar=sc, in1=in1, op0=op0, op1=op1), R, W)

        def CP(eng, out, in_, R, W):
            if eng == "act":
                S.op("act", lambda e: e.activation(out=out, in_=in_, func=AF.Copy), R, W)
            else:
                S.op(eng, lambda e: e.tensor_copy(out, in_), R, W)

        def CPP(eng, out, in_, R, W):
            if eng == "act":
                S.op("act", lambda e: e.activation(out=out, in_=in_, func=AF.Copy), R, W)
            else:
                S.op("dve", lambda e: e.tensor_scalar(out=out, in0=in_, scalar1=1.0, scalar2=None, op0=ALU.mult), R, W)

        def MSET(eng, ap, val, W):
            S.op(eng, lambda e: e.memset(ap, val), (), W)

        def DMA(q, out, in_, sem, R, W):
            S.dma(q, lambda e: e.dma_start(out=out, in_=in_), sem, R, W)

        uniq = [0]

        def sbuf(st, name, shape, dt):
            uniq[0] += 1
            return st.enter_context(nc.sbuf_tensor("%s_%d" % (name, uniq[0]), list(shape), dt))

        def psum(st, name, shape, dt=F32):
            uniq[0] += 1
            return st.enter_context(nc.psum_tensor("%s_%d" % (name, uniq[0]), list(shape), dt))

        def mkring(st, name, n, shape, dt, with_sem=False, arena=None):
            items = []
            for i in range(n):
                t = arena.alloc(shape, dt) if arena is not None else sbuf(st, "%s%d" % (name, i), shape, dt)
                b = Buf("%s%d" % (name, i))
                if with_sem:
                    items.append((t, b, S.new_dma_sem(name)))
                else:
                    items.append((t, b))
            return Ring(items)

        X = sbuf(es, "X", [128, 8, NX], F32)
        Xb = [Buf("X_%d" % i) for i in range(len(MAIN_TILES) + 1)]
        AR = Arena(X[:].rearrange("p a b -> p (a b)"), 8 * NX)

        def xbuf_for(c0, c1):
            res = []
            tl = MAIN_TILES + [(NXM, NSM)]
            for i, (a, n) in enumerate(tl):
                if a < c1 and c0 < a + n:
                    res.append(Xb[i])
            return res

        COLS = sbuf(es, "COLS", [128, NCOLS], F32)
        PCC = sbuf(es, "PCC", [128, 34], F32)
        CM = sbuf(es, "CM", [128, 6, 128], F32)
        O128 = sbuf(es, "O128", [128, 128], F32)
        O256 = sbuf(es, "O256", [128, 128], F32)
        O1024 = sbuf(es, "O1024", [128, 128], BF16)
        EPSC = sbuf(es, "EPSC", [128, 2], F32)
        ONEC = sbuf(es, "ONEC", [128, 1], F32)
        ONER = sbuf(es, "ONER", [1, 128], BF16)
        IDB = sbuf(es, "IDB", [128, 128], BF16)
        LBC = sbuf(es, "LBC", [128, 4], F32)
        OMLC = sbuf(es, "OMLC", [128, 4], F32)
        cst = Buf("consts")
        semc = S.new_dma_sem("c0")
        semc2 = S.new_dma_sem("c1")
        semc3 = S.new_dma_sem("c2")
        DMA("sp", COLS[:], cols_d, semc, [], [cst])
        DMA("sp", PCC[:], pcc_d, semc2, [], [cst])
        DMA("sp", CM[:], cmat_d.rearrange("m p c -> p m c"), semc3, [], [cst])
        MSET("dve", O128[:], 1.0 / 128, [cst])
        MSET("dve", O256[:], 1.0 / 256, [cst])
        MSET("dve", O1024[:], 1.0 / 1024, [cst])
        MSET("dve", ONEC[:], 1.0, [cst])
        MSET("dve", EPSC[:, 0:1], LN_EPS, [cst])
        MSET("dve", EPSC[:, 1:2], RMS_EPS, [cst])
        MSET("dve", ONER[:], 1.0, [cst])
        CP("dve", IDB[:], CM[:, 0, :], [cst], [cst])
        TT("dve", LBC[:], COLS[:, C_HGLB:C_HGLB + 4], COLS[:, C_HGLB + 4:C_HGLB + 8], ALU.subtract, [cst], [cst])
        ACT(LBC[:], LBC[:], AF.Sigmoid, [cst], [cst])
        TS("dve", OMLC[:], LBC[:], -1.0, 1.0, ALU.mult, ALU.add, [cst], [cst])
        IDF = CM[:, 0, :]
        TRI = CM[:, 1, :]
        UTM = CM[:, 2, :]
        TRIS = CM[0:64, 3, 0:64]
        UTS = CM[0:64, 4, 0:64]
        SEQM = CM[0:64, 5, 0:16]
        FLAG = PCC[:, 0:1]

        def colv(c):
            return COLS[:, c:c + 1]

        def layer_norm_x(st_ln, c0, n, gcol, bcol, xb):
            (ZB, zbB), (ZQ, zqB), (MS, msB), (VR, vrB), PM, pmB, PQ, pqB = st_ln
            Z = X[:, :, c0:c0 + n]
            CP("pool", ZB[:, :, 0:n], Z, xb, [zbB])
            ACT(ZQ[:, :, 0:n], Z, AF.Square, xb, [zqB])
            for kc in range(8):
                MM(PM[:, 0:n], O1024[:], ZB[:, kc, 0:n], kc == 0, kc == 7, [zbB, cst], [pmB])
            for kc in range(8):
                MM(PQ[:, 0:n], O1024[:], ZQ[:, kc, 0:n], kc == 0, kc == 7, [zqB, cst], [pqB])
            CP("act", MS[:, 0:n], PM[:, 0:n], [pmB], [msB])
            TT("pool", VR[:, 0:n], MS[:, 0:n], MS[:, 0:n], ALU.mult, [msB], [vrB])
            TT("dve", VR[:, 0:n], PQ[:, 0:n], VR[:, 0:n], ALU.subtract, [pqB, vrB], [vrB])
            RSQRT(VR[:, 0:n], VR[:, 0:n], EPSC[:, 0:1], [vrB, cst], [vrB])
            TT("dve", Z, Z, MS[:, 0:n].unsqueeze(1).to_broadcast([128, 8, n]), ALU.subtract, xb + [msB], xb)
            TT("pool", Z, Z, VR[:, 0:n].unsqueeze(1).to_broadcast([128, 8, n]), ALU.mult, xb + [vrB], xb)
            for kc in range(8):
                ACT(X[:, kc, c0:c0 + n], X[:, kc, c0:c0 + n], AF.Identity, xb + [cst], xb,
                    bias=colv(bcol + kc), scale=colv(gcol + kc))

        def alloc_ln(st):
            ZBr = (sbuf(st, "ZB", [128, 8, 512], BF16), Buf("ZB"))
            ZQr = (sbuf(st, "ZQ", [128, 8, 512], BF16), Buf("ZQ"))
            MSr = (sbuf(st, "MS", [128, 512], F32), Buf("MS"))
            VRr = (sbuf(st, "VR", [128, 512], F32), Buf("VR"))
            PM = psum(st, "PM", [128, 512])
            PQ = psum(st, "PQ", [128, 512])
            return (ZBr, ZQr, MSr, VRr, PM, Buf("PM"), PQ, Buf("PQ"))

        l0 = ExitStack()
        MIXHG = sbuf(l0, "MIXHG", [128, 4, NX], BF16)
        MIXML = sbuf(l0, "MIXML", [128, 4, NX], BF16)
        mixhgB = [Buf("mixhg%d" % i) for i in range(6)]
        mixmlB = [Buf("mixml%d" % i) for i in range(6)]

        with ExitStack() as st:
            AR.reset()
            WGr = mkring(st, "WG", 2, [128, 8, 512], BF16, True, arena=AR)
            XBr = mkring(st, "XB", 2, [128, 8, 512], BF16, True, arena=AR)
            LBR = sbuf(st, "LBR", [128, 512], F32)
            OMLR = sbuf(st, "OMLR", [128, 512], F32)
            TMPR = sbuf(st, "TMPR", [128, 512], F32)
            rowsB = Buf("rows")
            sr1 = S.new_dma_sem("r1")
            sr2 = S.new_dma_sem("r2")
            DMA("sp", LBR[:], rows_d[0].partition_broadcast(128), sr1, [], [rowsB])
            DMA("sp", TMPR[:], rows_d[1].partition_broadcast(128), sr2, [], [rowsB])
            TT("dve", LBR[:], LBR[:], TMPR[:], ALU.subtract, [rowsB], [rowsB])
            ACT(LBR[:], LBR[:], AF.Sigmoid, [rowsB], [rowsB])
            TS("dve", OMLR[:], LBR[:], -1.0, 1.0, ALU.mult, ALU.add, [rowsB], [rowsB])
            MASK4 = sbuf(st, "MASK4", [128, 4, 128], BF16)
            MASK4S = sbuf(st, "MASK4S", [64, 4, 64], BF16)
            sm1 = S.new_dma_sem("m1")
            sm2 = S.new_dma_sem("m2")
            DMA("pool", MASK4[:], mask8_d[:, 0:4, :], sm1, [], [cst])
            DMA("pool", MASK4S[:], mask4s_d, sm2, [], [cst])

            KINT = sbuf(st, "KINT", [128, 4, 512], BF16); kintB = Buf("KINT")
            HQT = sbuf(st, "HQT", [128, 4, 512], BF16); hqtB = Buf("HQT")
            SHG = sbuf(st, "SHG", [128, 4, 512], BF16); shgB = Buf("SHG")
            SGT = sbuf(st, "SGT", [128, 512], F32); sgtB = Buf("SGT")
            LF = AR.alloc([128, 4, 512], F32); lfB = [Buf("LF%d" % i) for i in range(4)]
            KINM = AR.alloc([128, 4, 512], F32); kinmB = [Buf("KINM%d" % i) for i in range(4)]
            VTM = sbuf(st, "VTM", [128, 4, 512], BF16); vtmB = [Buf("VTM%d" % i) for i in range(4)]
            SGM = sbuf(st, "SGM", [128, 512], F32); sgmB = Buf("SGM")
            NEGM = sbuf(st, "NEGM", [128, 4], F32); MPOS = sbuf(st, "MPOS", [128, 4], F32); nmB = Buf("NM")
            EQ = AR.alloc([128, 4, 128], F32); eqB = Buf("EQ")
            EK = AR.alloc([128, 4, 128], F32); ekB = Buf("EK")
            EB = AR.alloc([128, 4, 128], F32); ebB = Buf("EB")
            QS = sbuf(st, "QS", [128, 4, 128], BF16); qsB = Buf("QS")
            KS = sbuf(st, "KS", [128, 4, 128], BF16); ksB = Buf("KS")
            QB = sbuf(st, "QB", [128, 4, 128], BF16); qbB = Buf("QB")
            QBF = sbuf(st, "QBF", [128, 4, 64], F32); qbfB = Buf("QBF")
            ED = sbuf(st, "ED", [128, 512], F32); edB = Buf("ED")
            KH = sbuf(st, "KH", [128, 512], BF16); khB = Buf("KH")
            KHMr = mkring(st, "KHM", 2, [64, 512], BF16)
            EBE = sbuf(st, "EBE", [128, 4], F32); ebeB = Buf("EBE")
            PP = sbuf(st, "PP", [128, 4, 128], BF16); ppB = Buf("PP")
            SST = sbuf(st, "SST", [128, 4, 128], F32); sstB = Buf("SST")
            SBF = sbuf(st, "SBF", [128, 4, 128], BF16); sbfB = Buf("SBF")
            SQ = AR.alloc([128, 512], F32); sqB = Buf("SQ")
            RSTD = AR.alloc([128, 512], F32); rsB = Buf("RSTD")
            ON = AR.alloc([128, 512], F32); onB = Buf("ON")
            OSB = sbuf(st, "OSB", [128, 256], F32); osbB = Buf("OSB")
            SJr = mkring(st, "SJ", 2, [128, 4, 128], F32, True, arena=AR)
            SNr = mkring(st, "SN", 2, [128, 4, 128], F32, True, arena=AR)
            PA = psum(st, "PA", [128, 512]); paB = Buf("PA")
            PB = psum(st, "PB", [128, 512]); pbB = Buf("PB")
            PC = psum(st, "PC", [128, 512]); pcB = Buf("PC")
            PD = psum(st, "PD", [128, 512]); pdB = Buf("PD")
            PF = psum(st, "PF", [128, 512]); pfB = Buf("PF")
            PG = psum(st, "PG", [128, 512]); pgB = Buf("PG")
            PI = psum(st, "PI", [128, 512]); piB = Buf("PI")
            PE_ = psum(st, "PE_", [128, 512]); peB = Buf("PE_")
            sem_hgp = S.new_dma_sem("hgp")

            MSET("dve", SST[:], 0.0, [sstB])
            MSET("dve", SBF[:], 0.0, [sbfB])

            def load_x_bf(e0, n, sample=False):
                XB, xbB, xsem = XBr.next()
                src = xsTv[:, :, 0:n] if sample else xTv[:, :, e0:e0 + n]
                DMA("pool", XB[:, :, 0:n], src, xsem, [], [xbB])
                return XB, xbB

            def load_wg(dram, g):
                WG, wgB, wsem = WGr.next()
                DMA("pool", WG[:], dram[g], wsem, [], [wgB])
                return WG, wgB

            def proj_fm(WG, wgB, XB, xbB, fb, n, P, pB):
                for kc in range(8):
                    MM(P[:, 0:n], WG[:, kc, fb * 128:(fb + 1) * 128], XB[:, kc, 0:n], kc == 0, kc == 7, [wgB, xbB], [pB])

            def proj_tm(WG, wgB, XB, xbB, b0, nt, P, pB):
                for kc in range(8):
                    MM(P[0:nt, 0:512], XB[:, kc, b0:b0 + nt], WG[:, kc, 0:512], kc == 0, kc == 7, [wgB, xbB], [pB])

            def hgrn_tile(e0, n, kind, xcol0, mixb):
                nt = 64 if kind == "sample" else 128
                nblk = 1 if kind == "sample" else n // 128
                XB, xbB = load_x_bf(e0, n, sample=(kind == "sample"))
                WG, wgB = load_wg(wab_d, 1)
                if kind != "pre":
                    for fb in range(4):
                        proj_fm(WG, wgB, XB, xbB, fb, n, PA, paB)
                        ACT(SGT[:, 0:n], PA[:, 0:n], AF.Sigmoid, [paB], [sgtB], scale=-1.0)
                        TS("dve", KINT[:, fb, 0:n], SGT[:, 0:n], OMLC[:, fb:fb + 1], None, ALU.mult, None, [sgtB, cst], [kintB])
                for b in range(nblk):
                    proj_tm(WG, wgB, XB, xbB, b * 128, nt, PB, pbB)
                    ACT(SGM[0:nt, :], PB[0:nt, :], AF.Sigmoid, [pbB], [sgmB])
                    TT("dve", SGM[0:nt, :], SGM[0:nt, :], OMLR[0:nt, :], ALU.mult, [sgmB, rowsB], [sgmB])
                    TT("dve", SGM[0:nt, :], SGM[0:nt, :], LBR[0:nt, :], ALU.add, [sgmB, rowsB], [sgmB])
                    ACT(LF[0:nt, b, :], SGM[0:nt, :], AF.Ln, [sgmB], [lfB[b]])
                    TS("pool", KINM[0:nt, b, :], SGM[0:nt, :], -1.0, 1.0, ALU.mult, ALU.add, [sgmB], [kinmB[b]])
                WG, wgB = load_wg(wab_d, 2)
                for b in range(nblk):
                    proj_tm(WG, wgB, XB, xbB, b * 128, nt, PB, pbB)
                    CP("act", VTM[0:nt, b, :], PB[0:nt, :], [pbB], [vtmB[b]])
                if kind != "pre":
                    WG, wgB = load_wg(wab_d, 3)
                    for fb in range(4):
                        proj_fm(WG, wgB, XB, xbB, fb, n, PA, paB)
                        CP("act", HQT[:, fb, 0:n], PA[:, 0:n], [paB], [hqtB])
                    WG, wgB = load_wg(wab_d, 4)
                    for fb in range(4):
                        proj_fm(WG, wgB, XB, xbB, fb, n, PA, paB)
                        ACT(SHG[:, fb, 0:n], PA[:, 0:n], AF.Silu, [paB], [shgB])
                for b in range(nblk):
                    c0 = b * 128
                    ncol = nt
                    tri = TRIS if kind == "sample" else TRI
                    utm = UTS if kind == "sample" else UTM
                    for h in range(4):
                        MM(PE_[:, h:h + 1], LF[0:nt, b, h * 128:(h + 1) * 128], ONEC[0:nt, :], True, True, [lfB[b], cst], [peB])
                    if kind != "sample":
                        ACT(EBE[:], PE_[:, 0:4], AF.Exp, [peB], [ebeB])
                    MM(PD[0:nt, :], utm, LF[0:nt, b, :], True, True, [lfB[b], cst], [pdB])
                    ACT(ED[0:nt, :], PD[0:nt, :], AF.Exp, [pdB], [edB])
                    TT("dve", KH[0:nt, :], KINM[0:nt, b, :], ED[0:nt, :], ALU.mult, [kinmB[b], edB], [khB])
                    if kind != "pre":
                        PCv = PC[:, 0:4 * ncol].rearrange("p (h t) -> p h t", h=4)
                        for h in range(4):
                            MM(PCv[:, h, :], LF[0:nt, b, h * 128:(h + 1) * 128], tri, True, True, [lfB[b], cst], [pcB])
                        EQv = EQ[:, :, 0:ncol]
                        EKv = EK[:, :, 0:ncol]
                        EBv = EB[:, :, 0:ncol]
                        if kind == "main":
                            TS("dve", NEGM[:], PCv[:, :, 63], -1.0, None, ALU.mult, None, [pcB], [nmB])
                            CPP("dve", MPOS[:], PCv[:, :, 63], [pcB], [nmB])
                            for h in range(4):
                                ACT(EQv[:, h, :], PCv[:, h, :], AF.Exp, [pcB, nmB], [eqB], bias=NEGM[:, h:h + 1])
                                ACT(EKv[:, h, :], PCv[:, h, :], AF.Exp, [pcB, nmB], [ekB], bias=MPOS[:, h:h + 1], scale=-1.0)
                            ACT(EBv, PCv, AF.Exp, [pcB], [ebB])
                        else:
                            ACT(EBv, PCv, AF.Exp, [pcB], [ebB])
                            ACT(EKv, PCv, AF.Exp, [pcB], [ekB], scale=-1.0)
                            EQv = EBv
                            eqB_ = ebB
                        eqBuf = eqB if kind == "main" else ebB
                        TT("dve", QS[:, :, 0:ncol], HQT[:, :, c0:c0 + ncol], EQv, ALU.mult, [hqtB, eqBuf], [qsB])
                        TT("pool", KS[:, :, 0:ncol], KINT[:, :, c0:c0 + ncol], EKv, ALU.mult, [kintB, ekB], [ksB])
                        if kind == "main":
                            TT("dve", QB[:, :, 0:ncol], HQT[:, :, c0:c0 + ncol], EBv, ALU.mult, [hqtB, ebB], [qbB])
                        else:
                            TT("dve", QBF[:, :, 0:ncol], HQT[:, :, c0:c0 + ncol], EBv, ALU.mult, [hqtB, ebB], [qbfB])
                        PFv = PF[0:nt, 0:4 * ncol].rearrange("p (h t) -> p h t", h=4)
                        for h in range(4):
                            MM(PFv[:, h, :], KS[:, h, 0:nt], QS[:, h, 0:ncol], True, True, [ksB, qsB], [pfB])
                        msk = MASK4S[:] if kind == "sample" else MASK4[:]
                        TT("dve", PP[0:nt, :, 0:ncol], PFv, msk, ALU.mult, [pfB, cst], [ppB])
                        PGv = PG[:, 0:4 * ncol].rearrange("p (h t) -> p h t", h=4)
                        for h in range(4):
                            MM(PGv[:, h, :], VTM[0:nt, b, h * 128:(h + 1) * 128], PP[0:nt, h, 0:ncol], True, kind == "sample", [vtmB[b], ppB], [pgB])
                            if kind == "main":
                                MM(PGv[:, h, :], SBF[:, h, :], QB[:, h, 0:ncol], False, True, [sbfB, qbB], [pgB])
                    if kind == "sample":
                        for j in range(16):
                            SJ, sjB, sjsem = SJr.next()
                            DMA("sp", SJ[:], shg_d[j].rearrange("h k v -> k h v"), sjsem, [], [sjB])
                            PDv = PD[:, 0:256].rearrange("p (h t) -> p h t", h=4)
                            for h in range(4):
                                MM(PDv[:, h, 4 * j:4 * j + 4], SJ[:, h, :], QBF[:, h, 4 * j:4 * j + 4], True, True, [sjB, qbfB], [pdB])
                            KHM, khmB = KHMr.next()
                            TS("pool", KHM[:], KH[0:64, :], SEQM[:, j:j + 1], None, ALU.mult, None, [khB, cst], [khmB])
                            PFs = PF[:, :].rearrange("p (h v) -> p h v", h=4)
                            for h in range(4):
                                MM(PFs[:, h, :], KHM[:, h * 128:(h + 1) * 128], VTM[0:64, b, h * 128:(h + 1) * 128], True, True, [khmB, vtmB[b]], [pfB])
                            SN, snB, snsem = SNr.next()
                            for h in range(4):
                                STT("dve", SN[:, h, :], SJ[:, h, :], EB[:, h, 4 * j + 3:4 * j + 4], PFs[:, h, :], ALU.mult, ALU.add, [sjB, ebB, pfB], [snB])
                            DMA("sp", hgs_o[j].rearrange("h k v -> k h v"), SN[:], snsem, [snB], [])
                    else:
                        PFs = PF[:, :].rearrange("p (h v) -> p h v", h=4)
                        for h in range(4):
                            MM(PFs[:, h, :], KH[0:nt, h * 128:(h + 1) * 128], VTM[0:nt, b, h * 128:(h + 1) * 128], True, True, [khB, vtmB[b]], [pfB])
                        for h in range(4):
                            STT("dve", SST[:, h, :], SST[:, h, :], EBE[:, h:h + 1], PFs[:, h, :], ALU.mult, ALU.add, [sstB, ebeB, pfB], [sstB])
                        CP("pool", SBF[:], SST[:], [sstB], [sbfB])
                    if kind != "pre":
                        w = 4 * ncol
                        if kind == "sample":
                            CPP("act", OSB[:, 0:w], PG[:, 0:w], [pgB], [osbB])
                            TT("dve", OSB[:, 0:w], OSB[:, 0:w], PD[:, 0:w], ALU.add, [osbB, pdB], [osbB])
                            osrc, osB_ = OSB, osbB
                        else:
                            osrc, osB_ = PG, pgB
                        ACT(SQ[:, 0:w], osrc[:, 0:w], AF.Square, [osB_], [sqB])
                        MM(PI[:, 0:w], O128[:], SQ[:, 0:w], True, True, [sqB, cst], [piB])
                        RSQRT(RSTD[:, 0:w], PI[:, 0:w], EPSC[:, 1:2], [piB, cst], [rsB])
                        TT("dve", ON[:, 0:w], osrc[:, 0:w], RSTD[:, 0:w], ALU.mult, [osB_, rsB], [onB])
                        xc = xcol0 + c0
                        STT("dve", MIXHG[:, :, xc:xc + ncol], ON[:, 0:w].rearrange("p (h t) -> p h t", h=4), colv(C_GHG),
                            SHG[:, :, c0:c0 + ncol], ALU.mult, ALU.mult, [onB, shgB, cst], [mixb])

            for (e0, n) in PRE_TILES:
                hgrn_tile(e0, n, "pre", None, None)
            for ti, (m0, n) in enumerate(MAIN_TILES):
                hgrn_tile(EXT0 + m0, n, "main", m0, mixhgB[ti])
            DMA("sp", hgp_o.rearrange("h k v -> k h v"), SST[:], sem_hgp, [sstB], [])
            hgrn_tile(0, NSM, "sample", NXM, mixhgB[5])
            S.barrier()

        if DBG_PHASES >= 2:
          with ExitStack() as st:
            AR.reset()
            WGr = mkring(st, "WG", 2, [128, 8, 512], BF16, True, arena=AR)
            XBr = mkring(st, "XB", 2, [128, 8, 512], BF16, True, arena=AR)
            KT = AR.alloc([128, 4096], BF16); ktB = [Buf("KT%d" % i) for i in range(32)]
            KR = AR.alloc([128, 4096], BF16); krB = [Buf("KR%d" % i) for i in range(32)]
            CKA = AR.alloc([128, 32, 129], BF16); ckaB = [Buf("CKA%d" % i) for i in range(32)]
            KTs = sbuf(st, "KTs", [128, 64], BF16)
            KRs = sbuf(st, "KRs", [128, 64], BF16)
            CKAs = sbuf(st, "CKAs", [64, 129], BF16)
            ksB_ = Buf("ksample")
            WUQ = sbuf(st, "WUQ", [128, 2, 1024], BF16)
            WUKT = sbuf(st, "WUKT", [128, 8, 128], BF16)
            WUV = sbuf(st, "WUV", [128, 8, 128], BF16)
            MASK8 = sbuf(st, "MASK8", [128, 8, 128], BF16)
            MASK8S = sbuf(st, "MASK8S", [64, 512], BF16)
            RCr = mkring(st, "RC", 2, [32, 512], F32, True)
            RSr = mkring(st, "RS_", 2, [32, 512], F32, True)
            RCs = sbuf(st, "RCs", [32, 64], F32)
            RSs = sbuf(st, "RSs", [32, 64], F32)
            IDX = sbuf(st, "IDX", [128, 1024], I32); idxB = Buf("IDX")
            clat_rows = clat_d.rearrange("n p c -> (n p) c")
            ckr_rows = ckr_d.rearrange("n p c -> (n p) c")
            w2B = Buf("w2")
            for i, (dst, srcd, q) in enumerate([(WUQ[:], wuq_d, "pool"), (WUKT[:], wukt_d, "pool"), (WUV[:], wuv_d, "pool"),
                                                (MASK8[:], mask8_d, "pool"), (MASK8S[:], mask8s_d, "pool"),
                                                (RCs[:], ropeCs_d, "sp"),
                                                (RSs[:], ropeSs_d, "sp")]):
                DMA(q, dst, srcd, S.new_dma_sem("w2_%d" % i), [], [w2B])
            MSET("dve", KR[:], 0.0, krB)
            MSET("dve", KRs[:], 0.0, [ksB_])
            with ExitStack() as sti:
                PT32 = sbuf(sti, "PT32", [128, 1024], I32); pt32B = Buf("PT32")
                PTF = sbuf(sti, "PTF", [128, 1024], F32); ptfB = Buf("PTF")
                PTF2 = sbuf(sti, "PTF2", [128, 1024], F32); ptf2B = Buf("PTF2")
                DMA("sp", PT32[:], ptab_d[0].partition_broadcast(128), S.new_dma_sem("pt32"), [], [pt32B])
                CP("dve", PTF[:], PT32[:], [pt32B], [ptfB])
                TS("dve", PTF2[:], PTF[:], 128.0, PCC[:, 33:34], ALU.mult, ALU.add, [ptfB, cst], [ptf2B])
                CP("dve", IDX[:], PTF2[:], [ptf2B], [idxB])
                S.barrier()
            MSET("pool", CKA[:, :, 128:129], 1.0, ckaB)
            MSET("pool", CKAs[:, 128:129], 1.0, [ksB_])

            CQ = sbuf(st, "CQ", [128, 2, 512], F32); cqB = Buf("CQ")
            CQS = sbuf(st, "CQS", [128, 2, 512], F32); cqsB = Buf("CQS")
            RSQ = sbuf(st, "RSQ", [128, 512], F32); rsqB = Buf("RSQ")
            CQN = sbuf(st, "CQN", [128, 2, 512], BF16); cqnB = Buf("CQN")
            CK = sbuf(st, "CK", [128, 512], F32); ckB = Buf("CK")
            CKS = sbuf(st, "CKS", [128, 512], F32); cksB = Buf("CKS")
            CKN = sbuf(st, "CKN", [128, 512], F32); cknB = Buf("CKN")
            T1 = sbuf(st, "T1", [32, 512], F32); t1B = Buf("T1")
            T2 = sbuf(st, "T2", [32, 512], F32); t2B = Buf("T2")
            KRR = sbuf(st, "KRR", [32, 512], F32); krrB = Buf("KRR")
            LATOr = mkring(st, "LATO", 2, [128, 128], F32, True)
            KROr = mkring(st, "KRO", 2, [128, 32], F32, True)
            QN = sbuf(st, "QN", [128, 4, 512], BF16); qnB = Buf("QN")
            QR = AR.alloc([128, 8, 512], BF16); qrB = Buf("QR")
            QL = AR.alloc([128, 8, 512], BF16); qlB = Buf("QL")
            QRs = sbuf(st, "QRs", [128, 16, 8, 4], BF16)
            QLs = sbuf(st, "QLs", [128, 16, 8, 4], BF16)
            qsB_ = Buf("qsample")
            MSET("dve", QR[:], 0.0, [qrB])
            MSET("dve", QRs[:], 0.0, [qsB_])
            PTr = mkring(st, "PT", 2, [128, 8, 128], BF16)
            RDEN = sbuf(st, "RDEN", [128, 8], F32); rdB = Buf("RDEN")
            OL = sbuf(st, "OL", [128, 8, 128], BF16); olB = Buf("OL")
            OLT = sbuf(st, "OLT", [128, 8, 128], BF16); oltB = Buf("OLT")
            PGFr = mkring(st, "PGF", 2, [128, 4, 160], F32, True)
            PGBr = mkring(st, "PGB", 2, [128, 4, 161], BF16)
            CTr = mkring(st, "CT", 2, [128, 512], BF16)
            KRTr = mkring(st, "KRT", 2, [128, 512], BF16)
            PTPr = mkring(st, "PTP", 2, [128, 4, 32], BF16)
            PTN = sbuf(st, "PTN", [64, 512], BF16); ptnB = Buf("PTN")
            OLS = sbuf(st, "OLS", [32, 128], F32); olsB = Buf("OLS")
            OLTS = sbuf(st, "OLTS", [128, 32], BF16); oltsB = Buf("OLTS")
            RDS = sbuf(st, "RDS", [32, 1], F32); rdsB = Buf("RDS")
            for (t_, b_) in PGBr.items:
                MSET("pool", t_[:, :, 128:129], 1.0, [b_])
            for (t_, b_) in KRTr.items:
                MSET("pool", t_[:], 0.0, [b_])

            PA = psum(st, "PA2", [128, 512]); paB = Buf("PA")
            PB = psum(st, "PB2", [128, 512]); pbB = Buf("PB")
            PK = psum(st, "PK", [128, 512]); pkB = Buf("PK")
            PKS = psum(st, "PKS", [128, 512]); pksB = Buf("PKS")
            PT_ = psum(st, "PT_", [128, 512]); ptB_ = Buf("PT_")
            PKB = PT_[:, :].bitcast(BF16); pkbB = ptB_
            PO = [psum(st, "PO%d" % i, [128, 512]) for i in range(3)]
            poB = Buf("PO")

            def load_x_bf(e0, n, sample=False):
                XB, xbB, xsem = XBr.next()
                srcx = xsTv[:, :, 0:n] if sample else xTv[:, :, e0:e0 + n]
                DMA("pool", XB[:, :, 0:n], srcx, xsem, [], [xbB])
                return XB, xbB

            def rope_pair(Pa, paB_, Pb, pbB_, Cc, Ss, n, out, outB, R_extra):
                TT("dve", T1[:, 0:n], Pa[0:32, 0:n], Cc, ALU.mult, [paB_, w2B] + R_extra, [t1B])
                TT("dve", T2[:, 0:n], Pb[0:32, 0:n], Ss, ALU.mult, [pbB_, w2B] + R_extra, [t2B])
                TT("dve", out, T1[:, 0:n], T2[:, 0:n], ALU.add, [t1B, t2B], outB)

            def mla_tile(e0, n, kind, xcol0, ti):
                nblk = 1 if kind == "sample" else n // 128
                nt = 64 if kind == "sample" else 128
                XB, xbB = load_x_bf(e0, n, sample=(kind == "sample"))
                WG, wgB, wsem = WGr.next()
                DMA("pool", WG[:], wab_d[0], wsem, [], [wgB])

                def proj(fb0, width, P, pB_):
                    for kc in range(8):
                        MM(P[0:width, 0:n], WG[:, kc, fb0:fb0 + width], XB[:, kc, 0:n], kc == 0, kc == 7, [wgB, xbB], [pB_])

                if kind == "sample":
                    Cc = RCs[:, 0:n]
                    Ss = RSs[:, 0:n]
                    rtB = [w2B]
                else:
                    RCt, rcB, rcsem = RCr.next()
                    RSt, rsB_, rssem = RSr.next()
                    DMA("sp", RCt[:, 0:n], ropeC_d[:, e0:e0 + n], rcsem, [], [rcB])
                    DMA("sp", RSt[:, 0:n], ropeS_d[:, e0:e0 + n], rssem, [], [rsB_])
                    Cc = RCt[:, 0:n]
                    Ss = RSt[:, 0:n]
                    rtB = [rcB, rsB_]
                proj(256, 128, PA, paB)
                CP("act", CK[:, 0:n], PA[:, 0:n], [paB], [ckB])
                ACT(CKS[:, 0:n], PA[:, 0:n], AF.Square, [paB], [cksB])
                MM(PB[:, 0:n], O128[:], CKS[:, 0:n], True, True, [cksB, cst], [pbB])
                RSQRT(RSQ[:, 0:n], PB[:, 0:n], EPSC[:, 1:2], [pbB, cst], [rsqB])
                STT("dve", CKN[:, 0:n], CK[:, 0:n], colv(C_GKV), RSQ[:, 0:n], ALU.mult, ALU.mult, [ckB, rsqB, cst], [cknB])
                if kind == "sample":
                    CP("pool", KTs[:, 0:n], CKN[:, 0:n], [cknB], [ksB_])
                else:
                    kbs = list(range(e0 // 128, (e0 + n) // 128))
                    CP("pool", KT[:, e0:e0 + n], CKN[:, 0:n], [cknB], [ktB[k] for k in kbs])
                proj(384, 32, PK, pkB)
                proj(416, 32, PKS, pksB)
                rope_pair(PK, pkB, PKS, pksB, Cc, Ss, n, KRR[:, 0:n], [krrB], rtB)
                if kind == "sample":
                    CP("pool", KRs[0:32, 0:n], KRR[:, 0:n], [krrB], [ksB_])
                else:
                    CP("pool", KR[0:32, e0:e0 + n], KRR[:, 0:n], [krrB], [krB[k] for k in kbs])
                for b in range(nblk):
                    c0 = b * 128
                    TR(PT_[0:nt, 0:128], CKN[:, c0:c0 + nt], IDF, [cknB, cst], [ptB_])
                    TR(PT_[0:nt, 128:160], KRR[:, c0:c0 + nt], CM[0:32, 0, 0:32], [krrB, cst], [ptB_])
                    if kind == "sample":
                        CP("act", CKAs[:, 0:128], PT_[0:64, 0:128], [ptB_], [ksB_])
                    else:
                        kb = e0 // 128 + b
                        CP("act", CKA[:, kb, 0:128], PT_[:, 0:128], [ptB_], [ckaB[kb]])
                    own = (kind == "sample") or (kind == "main" and xcol0 + c0 >= 256)
                    if own:
                        LATO, latoB, lsem = LATOr.next()
                        KRO, kroB, ksem = KROr.next()
                        CP("act", LATO[0:nt, :], PT_[0:nt, 0:128], [ptB_], [latoB])
                        CP("act", KRO[0:nt, :], PT_[0:nt, 128:160], [ptB_], [kroB])
                        if kind == "sample":
                            DMA("sp", lats_o, LATO[0:64, :], lsem, [latoB], [])
                            DMA("sp", krs_o, KRO[0:64, :], ksem, [kroB], [])
                        else:
                            r0 = xcol0 + c0 - 256
                            if not DBG_SKIPLAT:
                                DMA("sp", latp_o[r0:r0 + 128, :], LATO[:], lsem, [latoB], [])
                            if not DBG_SKIPKR:
                                DMA("sp", krp_o[r0:r0 + 128, :], KRO[:], ksem, [kroB], [])
                if kind == "pre" or DBG_M < 1:
                    return
                for fb in range(2):
                    proj(fb * 128, 128, PA, paB)
                    CP("act", CQ[:, fb, 0:n], PA[:, 0:n], [paB], [cqB])
                    ACT(CQS[:, fb, 0:n], PA[:, 0:n], AF.Square, [paB], [cqsB])
                for fb in range(2):
                    MM(PB[:, 0:n], O256[:], CQS[:, fb, 0:n], fb == 0, fb == 1, [cqsB, cst], [pbB])
                RSQRT(RSQ[:, 0:n], PB[:, 0:n], EPSC[:, 1:2], [pbB, cst], [rsqB])
                for fb in range(2):
                    STT("dve", CQN[:, fb, 0:n], CQ[:, fb, 0:n], colv(C_GQ + fb), RSQ[:, 0:n], ALU.mult, ALU.mult, [cqB, rsqB, cst], [cqnB])
                if DBG_M < 2:
                    return
                for pb in range(4):
                    for kc in range(2):
                        MM(PA[:, 0:n], WUQ[:, kc, pb * 128:(pb + 1) * 128], CQN[:, kc, 0:n], kc == 0, kc == 1, [w2B, cqnB], [paB])
                    CP("act", QN[:, pb, 0:n], PA[:, 0:n], [paB], [qnB])
                if DBG_M < 3:
                    return
                for h in range(8):
                    for kc in range(2):
                        MM(PK[0:32, 0:n], WUQ[:, kc, 512 + h * 32:512 + (h + 1) * 32], CQN[:, kc, 0:n], kc == 0, kc == 1, [w2B, cqnB], [pkB])
                    for kc in range(2):
                        MM(PKS[0:32, 0:n], WUQ[:, kc, 768 + h * 32:768 + (h + 1) * 32], CQN[:, kc, 0:n], kc == 0, kc == 1, [w2B, cqnB], [pksB])
                    if kind == "sample":
                        outap = QRs[0:32, :, h, :]
                        TT("dve", T1[:, 0:n], PK[0:32, 0:n], Cc, ALU.mult, [pkB, w2B], [t1B])
                        TT("dve", T2[:, 0:n], PKS[0:32, 0:n], Ss, ALU.mult, [pksB, w2B], [t2B])
                        TT("dve", outap, T1[:, 0:n].rearrange("p (s t) -> p s t", t=4), T2[:, 0:n].rearrange("p (s t) -> p s t", t=4), ALU.add, [t1B, t2B], [qsB_])
                    else:
                        rope_pair(PK, pkB, PKS, pksB, Cc, Ss, n, QR[0:32, h, 0:n], [qrB], rtB)
                if DBG_M < 4:
                    return
                for h in range(8):
                    MM(PA[:, 0:n], WUKT[:, h, :], QN[:, h // 2, 0:n], True, True, [w2B, qnB], [paB])
                    if kind == "sample":
                        CP("act", QLs[:, :, h, :], PA[:, 0:n].rearrange("p (s t) -> p s t", t=4), [paB], [qsB_])
                    else:
                        CP("act", QL[:, h, 0:n], PA[:, 0:n], [paB], [qlB])
                if kind == "sample" or DBG_S2 < 3:
                    return
                for b in range(nblk):
                    c0 = b * 128
                    gi = (e0 + c0) // 128
                    for kb in range(gi + 1):
                        PT, ptB = PTr.next()
                        for half in range(2):
                            Pp, pB_ = (PA, paB) if half == 0 else (PB, pbB)
                            Pv = Pp[:, :].rearrange("p (h q) -> p h q", h=4)
                            MM(Pv, KT[:, kb * 128:(kb + 1) * 128], QL[:, 4 * half:4 * half + 4, c0:c0 + 128], True, False, [ktB[kb], qlB], [pB_])
                            MM(Pv, KR[:, kb * 128:(kb + 1) * 128], QR[:, 4 * half:4 * half + 4, c0:c0 + 128], False, True, [krB[kb], qrB], [pB_])
                            ACT(PT[:, 4 * half:4 * half + 4, :], Pv, AF.Exp, [pB_, cst], [ptB], bias=PCC[:, 1 + kb:2 + kb], scale=SCALE)
                        if kb == gi:
                            TT("dve", PT[:], PT[:], MASK8[:], ALU.mult, [ptB, w2B], [ptB])
                        for h in range(8):
                            MM(PO[h // 3][:, (h % 3) * 129:(h % 3) * 129 + 129], PT[:, h, :], CKA[:, kb, :], kb == 0, kb == gi, [ptB, ckaB[kb]], [poB])
                    for h in range(8):
                        TS("dve", RDEN[:, h:h + 1], PO[h // 3][:, (h % 3) * 129 + 128:(h % 3) * 129 + 129], 1e-30, None, ALU.max, None, [poB], [rdB])
                    S.op("dve", lambda e: e.reciprocal(RDEN[:], RDEN[:]), [rdB], [rdB])
                    for h in range(8):
                        TS("dve", OL[:, h, :], PO[h // 3][:, (h % 3) * 129:(h % 3) * 129 + 128], RDEN[:, h:h + 1], None, ALU.mult, None, [poB, rdB], [olB])
                    for half in range(2):
                        for hh in range(4):
                            h = half * 4 + hh
                            TR(PKB[:, hh * 128:(hh + 1) * 128], OL[:, h, :], IDB[:], [olB, cst], [pkbB])
                        CPP("dve", OLT[:, half * 4:half * 4 + 4, :], PKB[:, 0:512].rearrange("p (h q) -> p h q", h=4), [pkbB], [oltB])
                    Pm = PK[:, :].rearrange("p (a q) -> p a q", a=4)
                    for h in range(8):
                        MM(Pm[:, h // 2, :], WUV[:, h, :], OLT[:, h, :], h % 2 == 0, h % 2 == 1, [w2B, oltB], [pkB])
                    xc = xcol0 + c0
                    CP("act", MIXML[:, :, xc:xc + 128], Pm, [pkB], [mixmlB[ti]])


            if DBG_S2 >= 1:
                for (e0, n) in PRE_TILES:
                    mla_tile(e0, n, "pre", None, None)
            if DBG_S2 >= 2:
                for ti, (m0, n) in enumerate(MAIN_TILES):
                    mla_tile(EXT0 + m0, n, "main", m0, ti)
            if DBG_S2 >= 4:
                mla_tile(0, NSM, "sample", NXM, 5)

            NSEQ_ATT = 16 if DBG_S2 >= 5 else 0
            PvN = PA[0:64, :]
            MM(PvN, KTs[:, :], QLs[:].rearrange("p s h t -> p (s h t)"), True, False, [ksB_, qsB_], [paB])
            MM(PvN, KRs[:, :], QRs[:].rearrange("p s h t -> p (s h t)"), False, True, [ksB_, qsB_], [paB])
            ACT(PTN[:], PvN, AF.Exp, [paB], [ptnB], scale=SCALE)
            TT("dve", PTN[:], PTN[:], MASK8S[:], ALU.mult, [ptnB, w2B], [ptnB])
            for j in range(NSEQ_ATT):
                POj = PO[j % 2][0:32, 0:129]
                pojB = poB
                for bt in range(16):
                    PGF, pgfB, sem_a = PGFr.next()

                    def gath(e, j=j, bt=bt, PGF=PGF):
                        lst = []
                        for pg in range(4):
                            idx = j * 64 + bt * 4 + pg
                            off = bass.IndirectOffsetOnAxis(ap=IDX[:, idx:idx + 1], axis=0)
                            lst.append(e.indirect_dma_start(out=PGF[:, pg, 0:128], out_offset=None, in_=clat_rows, in_offset=off))
                            off2 = bass.IndirectOffsetOnAxis(ap=IDX[:, idx:idx + 1], axis=0)
                            lst.append(e.indirect_dma_start(out=PGF[:, pg, 128:160], out_offset=None, in_=ckr_rows, in_offset=off2))
                        return lst
                    S.dma("pool", gath, sem_a, [idxB], [pgfB], count=8)
                    PGB, pgbB = PGBr.next()
                    CP("pool", PGB[:, :, 0:128], PGF[:, :, 0:128], [pgfB], [pgbB])
                    CP("pool", PGB[:, :, 129:161], PGF[:, :, 128:160], [pgfB], [pgbB])
                    CT, ctB = CTr.next()
                    KRT, krtB = KRTr.next()
                    for pg in range(4):
                        TR(PKB[:, pg * 128:(pg + 1) * 128], PGB[:, pg, 0:128], IDB[:], [pgbB, cst], [pkbB])
                    CPP("dve", CT[:], PKB[:, 0:512], [pkbB], [ctB])
                    for pg in range(4):
                        TR(PKB[0:32, pg * 128:(pg + 1) * 128], PGB[:, pg, 129:161], IDB[:], [pgbB, cst], [pkbB])
                    CPP("act", KRT[0:32, :], PKB[0:32, 0:512], [pkbB], [krtB])
                    Psc = PK[:, 0:128].rearrange("p (g c) -> p g c", g=4)
                    qlj = QLs[:, j, :, :].rearrange("p h t -> p (h t)")
                    qrj = QRs[:, j, :, :].rearrange("p h t -> p (h t)")
                    for pg in range(4):
                        MM(Psc[:, pg, :], CT[:, pg * 128:(pg + 1) * 128], qlj, True, False, [ctB, qsB_], [pkB])
                        MM(Psc[:, pg, :], KRT[:, pg * 128:(pg + 1) * 128], qrj, False, True, [krtB, qsB_], [pkB])
                    PTP, ptpB = PTPr.next()
                    ACT(PTP[:], Psc, AF.Exp, [pkB], [ptpB], scale=SCALE)
                    for pg in range(4):
                        MM(POj, PTP[:, pg, :], PGB[:, pg, 0:129], bt == 0 and pg == 0, False, [ptpB, pgbB], [pojB])
                MM(POj, PTN[:, j * 32:(j + 1) * 32], CKAs[:, :], False, True, [ptnB, ksB_], [pojB])
                TS("dve", RDS[:], POj[:, 128:129], 1e-30, None, ALU.max, None, [pojB], [rdsB])
                S.op("dve", lambda e: e.reciprocal(RDS[:], RDS[:]), [rdsB], [rdsB])
                TS("dve", OLS[:], POj[:, 0:128], RDS[:, 0:1], None, ALU.mult, None, [pojB, rdsB], [olsB])
                TR(PT_[:, 0:32], OLS[:], CM[0:32, 0, 0:32], [olsB, cst], [ptB_])
                CP("act", OLTS[:], PT_[:, 0:32], [ptB_], [oltsB])
                Pm = PT_[:, 64:80].rearrange("p (a t) -> p a t", a=4)
                for h in range(8):
                    MM(Pm[:, h // 2, :], WUV[:, h, :], OLTS[:, h * 4:(h + 1) * 4], h % 2 == 0, h % 2 == 1, [w2B, oltsB], [ptB_])
                CP("act", MIXML[:, :, NXM + 4 * j:NXM + 4 * j + 4], Pm, [ptB_], [mixmlB[5]])
            S.barrier()

        tiles_all = MAIN_TILES + [(NXM, NSM)]
        if DBG_PHASES >= 3:
          with ExitStack() as st:
            WO = sbuf(st, "WO", [128, 8, 1024], BF16); woB = Buf("WO")
            DMA("pool", WO[:], woab_d, S.new_dma_sem("wo"), [], [woB])
            XFr = mkring(st, "XF", 2, [128, 8, 512], F32, True)
            lnst = alloc_ln(st)
            PY = Ring([(psum(st, "PY0", [128, 512]), Buf("PY0")), (psum(st, "PY1", [128, 512]), Buf("PY1"))])
            for ti, (m0, n) in enumerate(tiles_all):
                XF, xfB, xsem = XFr.next()
                srcx = xsTv[:, :, 0:n] if ti == 5 else xTv[:, :, EXT0 + m0:EXT0 + m0 + n]
                DMA("sp", XF[:, :, 0:n], srcx, xsem, [], [xfB])
                for d in range(8):
                    P, pB = PY.next()
                    for kc in range(8):
                        rhs = MIXML[:, kc, m0:m0 + n] if kc < 4 else MIXHG[:, kc - 4, m0:m0 + n]
                        MM(P[:, 0:n], WO[:, kc, d * 128:(d + 1) * 128], rhs, kc == 0, kc == 7, [woB, mixmlB[ti], mixhgB[ti]], [pB])
                    STT("dve", X[:, d, m0:m0 + n], XF[:, d, 0:n], ALPHA, P[:, 0:n], ALU.mult, ALU.add, [xfB, pB], [Xb[ti]])
                layer_norm_x(lnst, m0, n, C_LN1G, C_LN1B, [Xb[ti]])
            S.barrier()
        l0.close()

        def ffn_phase(l):
          with ExitStack() as st:
            WUr = mkring(st, "WU", 4, [128, 8, 128], BF16, True)
            WDr = mkring(st, "WD", 2, [128, 22, 128], BF16, True)
            X1B = sbuf(st, "X1B", [128, 8, 832], BF16); x1bB = Buf("X1B")
            HT = sbuf(st, "HT", [128, 22, 832], BF16); htB = Buf("HT")
            UAr = mkring(st, "UA", 2, [128, 770], F32)
            UBr = mkring(st, "UB", 2, [128, 770], F32)
            USA = sbuf(st, "USA", [128, 16, 6], F32); usaB = Buf("USA")
            USB = sbuf(st, "USB", [128, 16, 6], F32); usbB = Buf("USB")
            CA = sbuf(st, "CA", [128, 768], F32); caB = Buf("CA")
            CB = sbuf(st, "CB", [128, 768], F32); cbB = Buf("CB")
            SG = sbuf(st, "SG", [128, 768], F32); sgB = Buf("SG")
            HALO = sbuf(st, "HALO", [128, 44, 2], F32); haloB = [Buf("HALO%d" % i) for i in range(44)]
            FSI = sbuf(st, "FSI", [128, 44, 16, 2], F32); fsiB = Buf("FSI")
            FSO = sbuf(st, "FSO", [128, 44, 16, 2], F32); fsoB = Buf("FSO")
            MSET("dve", HALO[:], 0.0, haloB)
            DMA("sp", FSI[:], sffn_d[l], S.new_dma_sem("fsi"), [], [fsiB])
            lnst = alloc_ln(st)
            PUA = Ring([(psum(st, "PUA%d" % i, [128, 512]), Buf("PUA%d" % i)) for i in range(2)])
            PUB = Ring([(psum(st, "PUB%d" % i, [128, 512]), Buf("PUB%d" % i)) for i in range(2)])
            PD = Ring([(psum(st, "PD%d" % i, [128, 512]), Buf("PD%d" % i)) for i in range(2)])

            def fcw(j, f):
                return colv(C_FCW + (l * 3 + j) * 44 + f)

            for fi, (t0, t1) in enumerate(FFN_TILES):
                T = t1 - t0
                Tp = min(t1, NXM) - t0
                Ts = T - Tp
                xb = xbuf_for(t0, t1)
                CP("pool", X1B[:, :, 0:T], X[:, :, t0:t1], xb, [x1bB])
                chunks = [(c, min(512, Tp - c)) for c in range(0, Tp, 512)]
                for f in range(22):
                    WA, waB, was = WUr.next()
                    WB, wbB, wbs = WUr.next()
                    DMA("pool", WA[:], wup_d[l, f], was, [], [waB])
                    DMA("pool", WB[:], wup_d[l, 22 + f], wbs, [], [wbB])
                    UA, uaB = UAr.next()
                    UB, ubB = UBr.next()
                    CP("pool", UA[:, 0:2], HALO[:, f, :], [haloB[f]], [uaB])
                    CP("pool", UB[:, 0:2], HALO[:, 22 + f, :], [haloB[22 + f]], [ubB])
                    for (c0, n) in chunks:
                        PA_, paB_ = PUA.next()
                        PB_, pbB_ = PUB.next()
                        for kc in range(8):
                            MM(PA_[:, 0:n], WA[:, kc, :], X1B[:, kc, c0:c0 + n], kc == 0, kc == 7, [waB, x1bB], [paB_])
                        for kc in range(8):
                            MM(PB_[:, 0:n], WB[:, kc, :], X1B[:, kc, c0:c0 + n], kc == 0, kc == 7, [wbB, x1bB], [pbB_])
                        CP("act", UA[:, 2 + c0:2 + c0 + n], PA_[:, 0:n], [paB_], [uaB])
                        CP("act", UB[:, 2 + c0:2 + c0 + n], PB_[:, 0:n], [pbB_], [ubB])
                    if fi == 0:
                        TS("pool", UA[:, 256:258], UA[:, 256:258], FLAG, None, ALU.mult, None, [uaB, cst], [uaB])
                        TS("pool", UB[:, 256:258], UB[:, 256:258], FLAG, None, ALU.mult, None, [ubB, cst], [ubB])
                    CP("pool", HALO[:, f, :], UA[:, Tp:Tp + 2], [uaB], [haloB[f]])
                    CP("pool", HALO[:, 22 + f, :], UB[:, Tp:Tp + 2], [ubB], [haloB[22 + f]])
                    TS("dve", CA[:, 0:Tp], UA[:, 0:Tp], fcw(0, f), None, ALU.mult, None, [uaB, cst], [caB])
                    STT("dve", CA[:, 0:Tp], UA[:, 1:Tp + 1], fcw(1, f), CA[:, 0:Tp], ALU.mult, ALU.add, [uaB, caB, cst], [caB])
                    STT("dve", CA[:, 0:Tp], UA[:, 2:Tp + 2], fcw(2, f), CA[:, 0:Tp], ALU.mult, ALU.add, [uaB, caB, cst], [caB])
                    TS("dve", CB[:, 0:Tp], UB[:, 0:Tp], fcw(0, 22 + f), None, ALU.mult, None, [ubB, cst], [cbB])
                    STT("dve", CB[:, 0:Tp], UB[:, 1:Tp + 1], fcw(1, 22 + f), CB[:, 0:Tp], ALU.mult, ALU.add, [ubB, cbB, cst], [cbB])
                    STT("dve", CB[:, 0:Tp], UB[:, 2:Tp + 2], fcw(2, 22 + f), CB[:, 0:Tp], ALU.mult, ALU.add, [ubB, cbB, cst], [cbB])
                    ACT(SG[:, 0:Tp], CA[:, 0:Tp], AF.Silu, [caB], [sgB])
                    TT("pool", HT[:, f, 0:Tp], SG[:, 0:Tp], CB[:, 0:Tp], ALU.mult, [sgB, cbB], [htB])
                    if Ts:
                        PA_, paB_ = PUA.next()
                        PB_, pbB_ = PUB.next()
                        for kc in range(8):
                            MM(PA_[:, 0:64], WA[:, kc, :], X1B[:, kc, Tp:Tp + 64], kc == 0, kc == 7, [waB, x1bB], [paB_])
                        for kc in range(8):
                            MM(PB_[:, 0:64], WB[:, kc, :], X1B[:, kc, Tp:Tp + 64], kc == 0, kc == 7, [wbB, x1bB], [pbB_])
                        for (US, usB, Pp, ppB_, ff) in ((USA, usaB, PA_, paB_, f), (USB, usbB, PB_, pbB_, 22 + f)):
                            CP("act", US[:, :, 2:6], Pp[:, 0:64].rearrange("p (s t) -> p s t", t=4), [ppB_], [usB])
                            CP("pool", US[:, :, 0:2], FSI[:, ff, :, :], [fsiB], [usB])
                            CP("pool", FSO[:, ff, :, :], US[:, :, 4:6], [usB], [fsoB])
                        CAs = CA[:, 0:64].rearrange("p (s t) -> p s t", t=4)
                        CBs = CB[:, 0:64].rearrange("p (s t) -> p s t", t=4)
                        SGs = SG[:, 0:64].rearrange("p (s t) -> p s t", t=4)
                        for (Cc_, ccB, US, usB, ff) in ((CAs, caB, USA, usaB, f), (CBs, cbB, USB, usbB, 22 + f)):
                            TS("dve", Cc_, US[:, :, 0:4], fcw(0, ff), None, ALU.mult, None, [usB, cst], [ccB])
                            STT("dve", Cc_, US[:, :, 1:5], fcw(1, ff), Cc_, ALU.mult, ALU.add, [usB, ccB, cst], [ccB])
                            STT("dve", Cc_, US[:, :, 2:6], fcw(2, ff), Cc_, ALU.mult, ALU.add, [usB, ccB, cst], [ccB])
                        ACT(SGs, CAs, AF.Silu, [caB], [sgB])
                        TT("pool", HT[:, f, Tp:Tp + 64].rearrange("p (s t) -> p s t", t=4), SGs, CBs, ALU.mult, [sgB, cbB], [htB])
                allchunks = chunks + ([(Tp, 64)] if Ts else [])
                for d in range(8):
                    WD, wdB, wds = WDr.next()
                    DMA("pool", WD[:], wdn_d[l, d], wds, [], [wdB])
                    for (c0, n) in allchunks:
                        P, pB = PD.next()
                        for f in range(22):
                            MM(P[:, 0:n], WD[:, f, :], HT[:, f, c0:c0 + n], f == 0, f == 21, [wdB, htB], [pB])
                        STT("dve", X[:, d, t0 + c0:t0 + c0 + n], X[:, d, t0 + c0:t0 + c0 + n], ALPHA, P[:, 0:n], ALU.mult, ALU.add, xb + [pB], xb)
                for (c0, n) in allchunks:
                    layer_norm_x(lnst, t0 + c0, n, C_LN2G + 8 * l, C_LN2B + 8 * l, xb)
            DMA("sp", ffp_o[l], HALO[:], S.new_dma_sem("ffp"), haloB, [])
            DMA("sp", ffs_o[l], FSO[:], S.new_dma_sem("ffs"), [fsoB], [])
            S.barrier()

        def l1_phase():
          with ExitStack() as st:
            WGr = mkring(st, "WG", 2, [128, 8, 512], BF16, True)
            WO = sbuf(st, "WO1", [128, 8, 1024], BF16); woB = Buf("WO1")
            DMA("pool", WO[:], wocd_d, S.new_dma_sem("wo1"), [], [woB])
            XB = sbuf(st, "XB1", [128, 8, 512], BF16); xbB = Buf("XB1")
            U = sbuf(st, "U", [128, 4, 512], BF16); uB = Buf("U")
            GBT = sbuf(st, "GBT", [128, 4, 512], BF16); gbtB = Buf("GBT")
            GCT = sbuf(st, "GCT", [128, 4, 512], F32); gctB = Buf("GCT")
            GCH = sbuf(st, "GCH", [128, 4, 514], F32); gchB = Buf("GCH")
            GCHS = sbuf(st, "GCHS", [128, 4, 16, 6], F32); gchsB = Buf("GCHS")
            SCHALO = sbuf(st, "SCHALO", [128, 4, 2], F32); schB = Buf("SCHALO")
            SSI = sbuf(st, "SSI", [128, 4, 16, 2], F32); ssiB = Buf("SSI")
            SSO = sbuf(st, "SSO", [128, 4, 16, 2], F32); ssoB = Buf("SSO")
            CV = sbuf(st, "CV", [128, 512], F32); cvB = Buf("CV")
            MIX1 = sbuf(st, "MIX1", [128, 8, 512], BF16); mixB = Buf("MIX1")
            G1 = sbuf(st, "G1", [128, 512], F32); g1B = Buf("G1")
            G2 = sbuf(st, "G2", [128, 512], F32); g2B = Buf("G2")
            GV = sbuf(st, "GV", [128, 512], F32); gvB = Buf("GV")
            VN = sbuf(st, "VN", [128, 512], F32); vnB = Buf("VN")
            VNB = sbuf(st, "VNB", [128, 512], BF16); vnbB = Buf("VNB")
            STT_ = sbuf(st, "STATS", [128, 8], F32); stB = Buf("STATS")
            GMG = sbuf(st, "GMG", [128, 512], F32)
            GMB = sbuf(st, "GMB", [128, 512], F32)
            WMT = sbuf(st, "WMT", [128, 4, 128], BF16)
            WMTS = sbuf(st, "WMTS", [64, 4, 64], BF16)
            BSR = sbuf(st, "BSR", [1, 4, 128], BF16)
            BSRS = sbuf(st, "BSRS", [1, 4, 64], BF16)
            c1B = Buf("c1")
            DMA("sp", GMG[:], rows_d[2].partition_broadcast(128), S.new_dma_sem("gmg"), [], [c1B])
            DMA("sp", GMB[:], rows_d[3].partition_broadcast(128), S.new_dma_sem("gmb"), [], [c1B])
            DMA("sp", SSI[:], ssc_d, S.new_dma_sem("ssi"), [], [ssiB])
            DMA("pool", WMT[:], wmt_d, S.new_dma_sem("wmt"), [], [c1B])
            DMA("pool", WMTS[:], wmts_d, S.new_dma_sem("wmts"), [], [c1B])
            DMA("pool", BSR[:], bsr_d, S.new_dma_sem("bsr"), [], [c1B])
            DMA("pool", BSRS[:], bsrs_d, S.new_dma_sem("bsrs"), [], [c1B])
            MSET("dve", SCHALO[:], 0.0, [schB])
            lnst = alloc_ln(st)
            PA = psum(st, "PA3", [128, 512]); paB = Buf("PA3")
            PB = psum(st, "PB3", [128, 512]); pbB = Buf("PB3")
            PC = psum(st, "PC3", [128, 512]); pcB = Buf("PC3")
            PY = Ring([(psum(st, "PY2", [128, 512]), Buf("PY2")), (psum(st, "PY3", [128, 512]), Buf("PY3"))])
            sem_scp = S.new_dma_sem("scp")
            sem_gmv = S.new_dma_sem("gmv")

            def gelu(out, P, pB_, rows, n, outB):
                ACT(G1[0:rows, 0:n], P, AF.Square, [pB_], [g1B])
                TS("dve", G1[0:rows, 0:n], G1[0:rows, 0:n], 0.044715, 1.0, ALU.mult, ALU.add, [g1B], [g1B])
                TT("dve", G1[0:rows, 0:n], G1[0:rows, 0:n], P, ALU.mult, [g1B, pB_], [g1B])
                ACT(G2[0:rows, 0:n], G1[0:rows, 0:n], AF.Sigmoid, [g1B], [g2B], scale=1.5957691216057308)
                TT("dve", out, G2[0:rows, 0:n], P, ALU.mult, [g2B, pB_], outB)

            def scw(j, fb):
                return colv(C_SCW + j * 4 + fb)

            for ti, (m0, n) in enumerate(tiles_all):
                sample = (ti == 5)
                nt = 64 if sample else 128
                nblk = 1 if sample else n // 128
                xb = [Xb[ti]]
                CP("pool", XB[:, :, 0:n], X[:, :, m0:m0 + n], xb, [xbB])

                def load_wg(g):
                    WG, wgB, wsem = WGr.next()
                    DMA("pool", WG[:], wcd_d[g], wsem, [], [wgB])
                    return WG, wgB

                def proj_fm(WG, wgB, fb):
                    for kc in range(8):
                        MM(PA[:, 0:n], WG[:, kc, fb * 128:(fb + 1) * 128], XB[:, kc, 0:n], kc == 0, kc == 7, [wgB, xbB], [paB])

                WG, wgB = load_wg(0)
                for fb in range(4):
                    proj_fm(WG, wgB, fb)
                    gelu(U[:, fb, 0:n], PA[:, 0:n], paB, 128, n, [uB])
                WG, wgB = load_wg(1)
                for b in range(nblk):
                    c0 = b * 128
                    for kc in range(8):
                        MM(PB[0:nt, :], XB[:, kc, c0:c0 + nt], WG[:, kc, 0:512], kc == 0, kc == 7, [wgB, xbB], [pbB])
                    gelu(GV[0:nt, :], PB[0:nt, :], pbB, nt, 512, [gvB])
                    S.op("dve", lambda e, nt=nt: e.reduce_sum(STT_[0:nt, 0:1], GV[0:nt, :], AX.X), [gvB], [stB])
                    ACT(VN[0:nt, :], GV[0:nt, :], AF.Square, [gvB], [vnB, stB], accum=STT_[0:nt, 1:2])
                    TS("dve", STT_[0:nt, 2:3], STT_[0:nt, 0:1], 1.0 / 512, None, ALU.mult, None, [stB], [stB])
                    TT("dve", STT_[0:nt, 3:4], STT_[0:nt, 2:3], STT_[0:nt, 2:3], ALU.mult, [stB], [stB])
                    STT("dve", STT_[0:nt, 4:5], STT_[0:nt, 1:2], 1.0 / 512, STT_[0:nt, 3:4], ALU.mult, ALU.subtract, [stB], [stB])
                    RSQRT(STT_[0:nt, 5:6], STT_[0:nt, 4:5], EPSC[0:nt, 0:1], [stB, cst], [stB])
                    TS("dve", VN[0:nt, :], GV[0:nt, :], STT_[0:nt, 2:3], STT_[0:nt, 5:6], ALU.subtract, ALU.mult, [gvB, stB, vnB], [vnB])
                    TT("dve", VN[0:nt, :], VN[0:nt, :], GMG[0:nt, :], ALU.mult, [vnB, c1B], [vnB])
                    TT("dve", VN[0:nt, :], VN[0:nt, :], GMB[0:nt, :], ALU.add, [vnB, c1B], [vnB])
                    if sample:
                        DMA("sp", gmv_o, VN[0:64, :], sem_gmv, [vnB], [])
                    CP("pool", VNB[0:nt, :], VN[0:nt, :], [vnB], [vnbB])
                    PCv = PC[:, 0:4 * nt].rearrange("p (g t) -> p g t", g=4)
                    for g in range(4):
                        wm = WMTS[0:64, g, :] if sample else WMT[:, g, :]
                        br = BSRS[0:1, g, :] if sample else BSR[0:1, g, :]
                        MM(PCv[:, g, :], VNB[0:nt, g * 128:(g + 1) * 128], wm, True, False, [vnbB, c1B], [pcB])
                        MM(PCv[:, g, :], ONER[0:1, :], br, False, True, [c1B, cst], [pcB])
                    TT("dve", MIX1[:, 0:4, c0:c0 + nt], U[:, :, c0:c0 + nt], PCv, ALU.mult, [uB, pcB], [mixB])
                WG, wgB = load_wg(2)
                for fb in range(4):
                    proj_fm(WG, wgB, fb)
                    CP("act", GBT[:, fb, 0:n], PA[:, 0:n], [paB], [gbtB])
                WG, wgB = load_wg(3)
                for fb in range(4):
                    proj_fm(WG, wgB, fb)
                    CP("act", GCT[:, fb, 0:n], PA[:, 0:n], [paB], [gctB])
                WG, wgB = load_wg(4)
                for fb in range(4):
                    proj_fm(WG, wgB, fb)
                    if sample:
                        TT("dve", GCHS[:, fb, :, 2:6], GCT[:, fb, 0:64].rearrange("p (s t) -> p s t", t=4),
                           PA[:, 0:64].rearrange("p (s t) -> p s t", t=4), ALU.mult, [gctB, paB], [gchsB])
                    else:
                        TT("dve", GCH[:, fb, 2:2 + n], GCT[:, fb, 0:n], PA[:, 0:n], ALU.mult, [gctB, paB], [gchB])
                if sample:
                    CP("pool", GCHS[:, :, :, 0:2], SSI[:], [ssiB], [gchsB])
                    CP("pool", SSO[:], GCHS[:, :, :, 4:6], [gchsB], [ssoB])
                    for fb in range(4):
                        CVs = CV[:, 0:64].rearrange("p (s t) -> p s t", t=4)
                        TS("dve", CVs, GCHS[:, fb, :, 0:4], scw(0, fb), None, ALU.mult, None, [gchsB, cst], [cvB])
                        STT("dve", CVs, GCHS[:, fb, :, 1:5], scw(1, fb), CVs, ALU.mult, ALU.add, [gchsB, cvB, cst], [cvB])
                        STT("dve", CVs, GCHS[:, fb, :, 2:6], scw(2, fb), CVs, ALU.mult, ALU.add, [gchsB, cvB, cst], [cvB])
                        TT("pool", MIX1[:, 4 + fb, 0:64], GBT[:, fb, 0:64], CV[:, 0:64], ALU.mult, [gbtB, cvB], [mixB])
                else:
                    if ti == 1:
                        TS("dve", GCH[:, :, 0:2], SCHALO[:], FLAG, None, ALU.mult, None, [schB, cst], [gchB])
                    else:
                        CP("dve", GCH[:, :, 0:2], SCHALO[:], [schB], [gchB])
                    CP("pool", SCHALO[:], GCH[:, :, n:n + 2], [gchB], [schB])
                    for fb in range(4):
                        TS("dve", CV[:, 0:n], GCH[:, fb, 0:n], scw(0, fb), None, ALU.mult, None, [gchB, cst], [cvB])
                        STT("dve", CV[:, 0:n], GCH[:, fb, 1:n + 1], scw(1, fb), CV[:, 0:n], ALU.mult, ALU.add, [gchB, cvB, cst], [cvB])
                        STT("dve", CV[:, 0:n], GCH[:, fb, 2:n + 2], scw(2, fb), CV[:, 0:n], ALU.mult, ALU.add, [gchB, cvB, cst], [cvB])
                        TT("pool", MIX1[:, 4 + fb, 0:n], GBT[:, fb, 0:n], CV[:, 0:n], ALU.mult, [gbtB, cvB], [mixB])
                    if ti == 4:
                        DMA("sp", scp_o, SCHALO[:], sem_scp, [schB], [])
                for d in range(8):
                    P, pB = PY.next()
                    for kc in range(8):
                        MM(P[:, 0:n], WO[:, kc, d * 128:(d + 1) * 128], MIX1[:, kc, 0:n], kc == 0, kc == 7, [woB, mixB], [pB])
                    STT("dve", X[:, d, m0:m0 + n], X[:, d, m0:m0 + n], ALPHA, P[:, 0:n], ALU.mult, ALU.add, xb + [pB, xbB], xb)
                layer_norm_x(lnst, m0, n, C_LN1G + 8, C_LN1B + 8, xb)
            DMA("sp", scs_o, SSO[:], S.new_dma_sem("scs"), [ssoB], [])
            S.barrier()

        if DBG_PHASES >= 4:
            ffn_phase(0)
        if DBG_PHASES >= 5:
            l1_phase()
        if DBG_PHASES >= 6:
            ffn_phase(1)
        allx = list(Xb)
        DMA("sp", yT_o.rearrange("(kc p) n -> p kc n", p=128), X[:, :, 256:NXM], S.new_dma_sem("yT"), allx, [])
        DMA("sp", ysT_o.rearrange("(kc p) n -> p kc n", p=128), X[:, :, NXM:NX], S.new_dma_sem("ysT"), allx, [])
        S.barrier()
        S.emit()
    return nc


def _blk(W, ncol):
    K = W.shape[0]
    return np.ascontiguousarray(W.reshape(K // 128, 128, ncol).transpose(1, 0, 2))


def _prep_shared(inp):
    f32 = np.float32
    sh = {}
    w = inp["w_in_ab"][0]
    cq, ckv, kr = w[:, 0:256], w[:, 256:384], w[:, 384:416]
    hf, hi, hq, hg = w[:, 416:928], w[:, 928:1440], w[:, 1440:1952], w[:, 1952:2464]
    perm = (np.arange(32) + 16) % 32
    G0 = np.concatenate([cq, ckv, kr, kr[:, perm], np.zeros((1024, 64), f32)], axis=1)
    sh["wab"] = np.stack([_blk(g, 512) for g in (G0, hf, hi, hq, hg)]).astype(f32)
    wq = inp["w_uq"][0].reshape(256, 8, 96)
    nope = wq[:, :, 0:64].reshape(256, 512)
    rope = wq[:, :, 64:96].reshape(256, 256)
    ropes = wq[:, :, 64:96][:, :, perm].reshape(256, 256)
    sh["wuq"] = _blk(np.concatenate([nope, rope, ropes], axis=1), 1024).astype(f32)
    wuk = inp["w_uk"][0]
    wuv = inp["w_uv"][0]
    wukt = np.zeros((128, 8, 128), f32)
    wuvp = np.zeros((128, 8, 128), f32)
    for h in range(8):
        o = (h % 2) * 64
        wukt[o:o + 64, h, :] = wuk[:, h, :].T
        wuvp[:, h, o:o + 64] = wuv[:, h, :]
    sh["wukt"] = wukt
    sh["wuv"] = wuvp
    sh["woab"] = _blk(inp["w_out_ab"][0], 1024).astype(f32)
    wc = inp["w_in_cd"][0]
    sh["wcd"] = np.stack([_blk(wc[:, g * 512:(g + 1) * 512], 512) for g in range(5)]).astype(f32)
    sh["wocd"] = _blk(inp["w_out_cd"][0], 1024).astype(f32)
    ws = inp["gm_ws"][0]
    tril = np.tril(np.ones((128, 128), bool))
    wm = np.where(tril[None], ws, 0).astype(f32)
    sh["wmt"] = np.ascontiguousarray(wm.transpose(2, 0, 1))
    wmts = np.zeros((64, 4, 64), f32)
    for j in range(16):
        for s_ in range(4):
            for t_ in range(s_, 4):
                wmts[4 * j + s_, :, 4 * j + t_] = ws[:, t_, s_]
    sh["wmts"] = wmts
    bs = inp["gm_bs"][0]
    sh["bsr"] = np.ascontiguousarray(bs[None]).astype(f32)
    sh["bsrs"] = np.ascontiguousarray(np.tile(bs[:, 0:4], (1, 16))[None]).astype(f32)
    wu = inp["ffn_w_up"]
    sh["wup"] = np.ascontiguousarray(wu.reshape(2, 8, 128, 44, 128).transpose(0, 3, 2, 1, 4)).astype(f32)
    wd = inp["ffn_w_down"]
    sh["wdn"] = np.ascontiguousarray(wd.reshape(2, 22, 128, 8, 128).transpose(0, 3, 2, 1, 4)).astype(f32)
    cols = np.zeros((128, NCOLS), f32)
    cols[:, C_GQ:C_GQ + 2] = inp["g_qnorm"][0].reshape(2, 128).T
    cols[:, C_GKV] = inp["g_kvnorm"][0]
    cols[:, C_GHG] = inp["g_hgnorm"][0]
    cols[:, C_HGLB:C_HGLB + 8] = inp["hg_lb"].reshape(2, 4, 128).transpose(2, 0, 1).reshape(128, 8)
    for nm, c0 in (("ln1_g", C_LN1G), ("ln1_b", C_LN1B), ("ln2_g", C_LN2G), ("ln2_b", C_LN2B)):
        cols[:, c0:c0 + 16] = inp[nm].reshape(2, 8, 128).transpose(2, 0, 1).reshape(128, 16)
    cols[:, C_FCW:C_FCW + 264] = inp["ffn_conv"].reshape(2, 3, 44, 128).transpose(3, 0, 1, 2).reshape(128, 264)
    cols[:, C_SCW:C_SCW + 12] = inp["sc_conv"][0].reshape(3, 4, 128).transpose(2, 0, 1).reshape(128, 12)
    sh["cols"] = cols
    sh["rows"] = np.ascontiguousarray(np.stack([inp["hg_lb"][0], inp["hg_lb"][1], inp["gm_ln_g"][0], inp["gm_ln_b"][0]])).astype(f32)
    cm = np.zeros((6, 128, 128), f32)
    i = np.arange(128)
    cm[0] = np.eye(128, dtype=f32)
    cm[1] = (i[:, None] <= i[None, :])
    cm[2] = (i[:, None] > i[None, :])
    k = np.arange(64)
    same = (k[:, None] // 4) == (k[None, :] // 4)
    cm[3, :64, :64] = same & (k[:, None] <= k[None, :])
    cm[4, :64, :64] = same & (k[:, None] > k[None, :])
    cm[5, :64, :16] = (k[:, None] // 4) == np.arange(16)[None, :]
    sh["cmat"] = cm
    sh["mask8"] = np.ascontiguousarray(np.broadcast_to(cm[1][:, None, :], (128, 8, 128))).astype(f32)
    m8s = np.zeros((64, 16, 8, 4), f32)
    for kk in range(64):
        for t_ in range(4):
            if (kk % 4) <= t_:
                m8s[kk, kk // 4, :, t_] = 1.0
    sh["mask8s"] = m8s.reshape(64, 512)
    sh["mask4s"] = np.ascontiguousarray(np.broadcast_to(cm[3, :64, None, :64], (64, 4, 64))).astype(f32)
    if not DBG_COMPACT:
        sh["clat"] = np.ascontiguousarray(inp["cache_latent"][0])
        sh["ckr"] = np.ascontiguousarray(inp["cache_krope"][0])
    return sh


def _rope_tabs(pos):
    f32 = np.float32
    inv = (f32(10000.0) ** (-(np.arange(0, 32, 2, dtype=f32)) / f32(32))).astype(f32)
    ang = (pos.astype(f32)[:, None] * inv[None, :]).astype(f32)
    c = np.cos(ang).astype(f32).T
    s = np.sin(ang).astype(f32).T
    return np.ascontiguousarray(np.concatenate([c, c], 0)), np.ascontiguousarray(np.concatenate([-s, s], 0))


def _prep_core(inp, c):
    f32 = np.float32
    d = {}
    s, hf = c // 2, c % 2
    pos0 = 2048 * hf - 2048
    xT = np.zeros((1024, 4096), f32)
    v0 = max(0, -pos0)
    xT[:, v0:] = inp["x_prompt"][s, pos0 + v0:pos0 + 4096, :].T
    d["xT"] = xT
    d["xsT"] = np.ascontiguousarray(inp["x_sample"][16 * c:16 * c + 16].reshape(64, 1024).T)
    pos = pos0 + np.arange(4096)
    d["ropeC"], d["ropeS"] = _rope_tabs(pos)
    d["ropeCs"], d["ropeSs"] = _rope_tabs(8192 + (np.arange(64) % 4))
    pcc = np.zeros((128, 34), f32)
    pcc[:, 33] = np.arange(128)
    pcc[:, 0] = float(hf)
    kb = np.where(pos >= 0, 0.0, -30000.0).astype(f32).reshape(32, 128).T
    pcc[:, 1:33] = kb
    d["pcc"] = pcc
    d["ptab"] = np.ascontiguousarray(inp["page_table"][16 * c:16 * c + 16].reshape(1, 1024)).astype(np.int32)
    if DBG_COMPACT:
        pt = d["ptab"][0]
        d["clat"] = np.ascontiguousarray(inp["cache_latent"][0][pt])
        d["ckr"] = np.ascontiguousarray(inp["cache_krope"][0][pt])
        d["ptab"] = np.arange(1024, dtype=np.int32).reshape(1, 1024)
    d["shg"] = np.ascontiguousarray(inp["state_hgrn"][0, 16 * c:16 * c + 16])
    st = inp["state_shortconv"][0, 16 * c:16 * c + 16]
    d["ssc"] = np.ascontiguousarray(st.transpose(2, 0, 1).reshape(4, 128, 16, 2).transpose(1, 0, 2, 3))
    sf = inp["state_ffn_conv"][:, 16 * c:16 * c + 16]
    d["sffn"] = np.ascontiguousarray(sf.transpose(0, 3, 1, 2).reshape(2, 44, 128, 16, 2).transpose(0, 2, 1, 3, 4))
    return d


_PROG = {}


def kernel(**inputs):
    inp = {k: np.asarray(v) for k, v in inputs.items()}
    if "nc" not in _PROG:
        _PROG["nc"] = build_program()
    nc = _PROG["nc"]
    sh = _prep_shared(inp)
    in_maps = []
    cores = list(range(8)) if DBG_CORES is None else list(DBG_CORES)
    for c in cores:
        d = dict(sh)
        d.update(_prep_core(inp, c))
        in_maps.append(d)
    res = run_bass_kernel_spmd(nc, in_maps, core_ids=list(range(len(cores))))
    R = {c: res.results[i] for i, c in enumerate(cores)}
    f32 = np.float32
    y_p = np.zeros((4, 4096, 1024), f32)
    y_s = np.zeros((128, 4, 1024), f32)
    lat_p = np.zeros((1, 4, 4096, 128), f32)
    kr_p = np.zeros((1, 4, 4096, 32), f32)
    lat_s = np.zeros((1, 128, 4, 128), f32)
    kr_s = np.zeros((1, 128, 4, 32), f32)
    hg_p = np.zeros((1, 4, 4, 128, 128), f32)
    hg_s = np.zeros((1, 128, 4, 128, 128), f32)
    gmv = np.zeros((1, 128, 4, 512), f32)
    sc_p = np.zeros((1, 4, 2, 512), f32)
    sc_s = np.zeros((1, 128, 2, 512), f32)
    ff_p = np.zeros((2, 4, 2, 5632), f32)
    ff_s = np.zeros((2, 128, 2, 5632), f32)
    for c in cores:
        r = R[c]
        s, hf = c // 2, c % 2
        h0 = 2048 * hf
        js = slice(16 * c, 16 * c + 16)
        y_p[s, h0:h0 + 2048] = r["yT"].T
        y_s[js] = r["ysT"].T.reshape(16, 4, 1024)
        lat_p[0, s, h0:h0 + 2048] = r["latp"]
        kr_p[0, s, h0:h0 + 2048] = r["krp"]
        lat_s[0, js] = r["lats"].reshape(16, 4, 128)
        kr_s[0, js] = r["krs"].reshape(16, 4, 32)
        hg_s[0, js] = r["hgs"]
        gmv[0, js] = r["gmv"].reshape(16, 4, 512)
        sc_s[0, js] = r["scs"].transpose(2, 3, 1, 0).reshape(16, 2, 512)
        for l in range(2):
            ff_s[l, js] = r["ffs"][l].transpose(2, 3, 1, 0).reshape(16, 2, 5632)
        if hf == 1:
            hg_p[0, s] = r["hgp"]
            sc_p[0, s] = r["scp"].transpose(2, 1, 0).reshape(2, 512)
            for l in range(2):
                ff_p[l, s] = r["ffp"][l].transpose(2, 1, 0).reshape(2, 5632)
    return (y_p, y_s, lat_p, kr_p, lat_s, kr_s, hg_p, hg_s, gmv, sc_p, sc_s, ff_p, ff_s)
```
